# Optimizing a Trainium2 kernel written in Bass

```python
import jax, jax.numpy as jnp
from jax import lax
import numpy as np

D_MODEL = 1024
BATCH = 4
SEQ = 8192
DEPTH = 2
DEC_BATCH = 32
DEC_SEQ = 64
PAST_LEN = 4096

CHUNK = 64
HEAD_DIM = 64
N_Q_A = 8
N_KV_A = 2
WINDOW = 128
N_PAST_A = WINDOW // CHUNK
N_HEADS_B = 8
N_PAST_B = 8
REL_CLIP = 128
ROT_DIM = HEAD_DIM // 4
ROPE_THETA = 500000.0
D_FF = 2816
NORM_EPS = 1e-5
MASK_VALUE = -1e30
Q_A_W = N_Q_A * HEAD_DIM
KV_A_W = N_KV_A * HEAD_DIM
Q_B_W = N_HEADS_B * HEAD_DIM
D_IN = Q_A_W + 2 * KV_A_W + 3 * Q_B_W + 2 * D_MODEL

kernel_name = 'streaming_hybrid_swa_chunkband'


def rms_norm(x, g):
    xf = x.astype(jnp.float32)
    r = lax.rsqrt(jnp.mean(xf * xf, axis=-1, keepdims=True) + NORM_EPS)
    return (xf * r).astype(x.dtype) * g


def swiglu(h, w_in, w_out):
    z = jnp.einsum('btd,df->btf', h, w_in)
    a, b = jnp.split(z, 2, axis=-1)
    return jnp.einsum('btf,fd->btd', jax.nn.silu(a) * b, w_out)


def apply_rope(x, pos):
    half = ROT_DIM // 2
    freqs = ROPE_THETA ** (-jnp.arange(0, ROT_DIM, 2, dtype=jnp.float32) / ROT_DIM)
    ang = pos.astype(jnp.float32)[:, None] * freqs[None, :]
    cos = jnp.cos(ang)[None, :, None, :]
    sin = jnp.sin(ang)[None, :, None, :]
    xf = x.astype(jnp.float32)
    x1, x2, rest = xf[..., :half], xf[..., half:ROT_DIM], xf[..., ROT_DIM:]
    out = jnp.concatenate([x1 * cos - x2 * sin, x2 * cos + x1 * sin, rest], axis=-1)
    return out.astype(x.dtype)


def project(h, w_in, b_gate, pos):
    nb, t, _ = h.shape
    z = jnp.einsum('btd,de->bte', h, w_in)
    idx = [int(v) for v in np.cumsum([Q_A_W, KV_A_W, KV_A_W, Q_B_W, Q_B_W, Q_B_W, D_MODEL])]
    qa, ka, va, qb, kb, vb, g_a, g_b = jnp.split(z, idx, axis=-1)
    qa = apply_rope(qa.reshape(nb, t, N_Q_A, HEAD_DIM), pos)
    ka = apply_rope(ka.reshape(nb, t, N_KV_A, HEAD_DIM), pos)
    va = va.reshape(nb, t, N_KV_A, HEAD_DIM)
    qb = qb.reshape(nb, t, N_HEADS_B, HEAD_DIM)
    kb = kb.reshape(nb, t, N_HEADS_B, HEAD_DIM)
    vb = vb.reshape(nb, t, N_HEADS_B, HEAD_DIM)
    gates = jax.nn.sigmoid(jnp.concatenate([g_a, g_b], axis=-1) + b_gate)
    gate_a, gate_b = jnp.split(gates, 2, axis=-1)
    return qa, ka, va, qb, kb, vb, gate_a, gate_b


def attend(q, k, v, qpos, kpos, kvalid, sink, rel_table):
    nb, tq, hq, hd = q.shape
    tk, hkv = k.shape[1], k.shape[2]
    g = hq // hkv
    qg = q.reshape(nb, tq, hkv, g, hd)
    s = jnp.einsum('bqhgd,bkhd->bhgqk', qg, k).astype(jnp.float32) * (hd ** -0.5)
    if rel_table is not None:
        rel = jnp.clip(kpos[None, :] - qpos[:, None], -REL_CLIP, REL_CLIP) + REL_CLIP
        s = s + rel_table[:, rel].astype(jnp.float32).reshape(hkv, g, tq, tk)
    if kvalid is not None:
        s = jnp.where(kvalid, s, MASK_VALUE)
    if sink is not None:
        sk = sink.astype(jnp.float32).reshape(1, hkv, g, 1, 1)
        m = jnp.maximum(jnp.max(s, axis=-1, keepdims=True), sk)
        p = jnp.exp(s - m)
        p = p / (jnp.sum(p, axis=-1, keepdims=True) + jnp.exp(sk - m))
    else:
        p = jax.nn.softmax(s, axis=-1)
    o = jnp.einsum('bhgqk,bkhd->bqhgd', p.astype(v.dtype), v)
    return o.reshape(nb, tq, hq, hd)


def band_prompt(q, k, v, n_past, sink, rel_table):
    nb, s_len, hq, hd = q.shape
    nc = s_len // CHUNK
    pad = n_past * CHUNK
    band = pad + CHUNK
    kp = jnp.pad(k, ((0, 0), (pad, 0), (0, 0), (0, 0)))
    vp = jnp.pad(v, ((0, 0), (pad, 0), (0, 0), (0, 0)))

    def one_chunk(c):
        start = c * CHUNK
        qc = lax.dynamic_slice_in_dim(q, start, CHUNK, axis=1)
        kc = lax.dynamic_slice_in_dim(kp, start, band, axis=1)
        vc = lax.dynamic_slice_in_dim(vp, start, band, axis=1)
        qpos = start + jnp.arange(CHUNK, dtype=jnp.int32)
        kpos = start - pad + jnp.arange(band, dtype=jnp.int32)
        return attend(qc, kc, vc, qpos, kpos, kpos >= 0, sink, rel_table)

    out = lax.map(one_chunk, jnp.arange(nc, dtype=jnp.int32))
    return jnp.transpose(out, (1, 0, 2, 3, 4)).reshape(nb, s_len, hq, hd)


def merge(ya, yb, gate_a, gate_b, w_branch_a, w_branch_b, w_mix_out):
    nb, t = ya.shape[0], ya.shape[1]
    ua = jnp.einsum('bte,ed->btd', ya.reshape(nb, t, Q_A_W), w_branch_a)
    ub = jnp.einsum('bte,ed->btd', yb.reshape(nb, t, Q_B_W), w_branch_b)
    return jnp.einsum('btd,de->bte', gate_a * ua + gate_b * ub, w_mix_out)


def setup_inputs(seed: int = 0) -> dict:
    key = jax.random.key(seed)
    ks = jax.random.split(key, 24)

    def nrm(k, shape, scale):
        return jax.random.normal(k, shape, jnp.float32) * scale

    cache_a = min(WINDOW, PAST_LEN)
    cache_b = min(N_PAST_B * CHUNK, PAST_LEN)
    return {
        'x_prompt': nrm(ks[0], (BATCH, SEQ, D_MODEL), 1.0),
        'x_sample': nrm(ks[1], (DEC_BATCH, DEC_SEQ, D_MODEL), 1.0),
        'cache_k_win': nrm(ks[2], (DEPTH, DEC_BATCH, cache_a, N_KV_A, HEAD_DIM), 1.0),
        'cache_v_win': nrm(ks[3], (DEPTH, DEC_BATCH, cache_a, N_KV_A, HEAD_DIM), 1.0),
        'cache_k_band': nrm(ks[4], (DEPTH, DEC_BATCH, cache_b, N_HEADS_B, HEAD_DIM), 1.0),
        'cache_v_band': nrm(ks[5], (DEPTH, DEC_BATCH, cache_b, N_HEADS_B, HEAD_DIM), 1.0),
        'norm_ffn1': 1.0 + nrm(ks[6], (DEPTH, D_MODEL), 0.02),
        'w_ffn1_in': nrm(ks[7], (DEPTH, D_MODEL, 2 * D_FF), D_MODEL ** -0.5),
        'w_ffn1_out': nrm(ks[8], (DEPTH, D_FF, D_MODEL), D_FF ** -0.5),
        'norm_mix': 1.0 + nrm(ks[9], (DEPTH, D_MODEL), 0.02),
        'w_mix_in': nrm(ks[10], (DEPTH, D_MODEL, D_IN), D_MODEL ** -0.5),
        'b_gate': nrm(ks[11], (DEPTH, 2 * D_MODEL), 0.02),
        'sinks': nrm(ks[12], (DEPTH, N_Q_A), 0.5),
        'rel_bias': nrm(ks[13], (DEPTH, N_HEADS_B, 2 * REL_CLIP + 1), 0.1),
        'w_branch_a': nrm(ks[14], (DEPTH, Q_A_W, D_MODEL), Q_A_W ** -0.5),
        'w_branch_b': nrm(ks[15], (DEPTH, Q_B_W, D_MODEL), Q_B_W ** -0.5),
        'w_mix_out': nrm(ks[16], (DEPTH, D_MODEL, D_MODEL), D_MODEL ** -0.5),
        'norm_ffn2': 1.0 + nrm(ks[17], (DEPTH, D_MODEL), 0.02),
        'w_ffn2_in': nrm(ks[18], (DEPTH, D_MODEL, 2 * D_FF), D_MODEL ** -0.5),
        'w_ffn2_out': nrm(ks[19], (DEPTH, D_FF, D_MODEL), D_FF ** -0.5),
        'norm_final': 1.0 + nrm(ks[20], (D_MODEL,), 0.02),
    }


def reference(x_prompt, x_sample, cache_k_win, cache_v_win, cache_k_band, cache_v_band,
              norm_ffn1, w_ffn1_in, w_ffn1_out, norm_mix, w_mix_in, b_gate, sinks, rel_bias,
              w_branch_a, w_branch_b, w_mix_out, norm_ffn2, w_ffn2_in, w_ffn2_out, norm_final):
    s_len = x_prompt.shape[1]
    t_len = x_sample.shape[1]
    len_a = cache_k_win.shape[2]
    len_b = cache_k_band.shape[2]
    keep_a = min(WINDOW, s_len)
    keep_b = min(N_PAST_B * CHUNK, s_len)
    pos_p = jnp.arange(s_len, dtype=jnp.int32)
    pos_s = PAST_LEN + jnp.arange(t_len, dtype=jnp.int32)
    kpos_a = PAST_LEN - len_a + jnp.arange(len_a + t_len, dtype=jnp.int32)
    kpos_b = PAST_LEN - len_b + jnp.arange(len_b + t_len, dtype=jnp.int32)

    xp, xs = x_prompt, x_sample
    kwp, vwp, kbp, vbp, kws, vws, kbs, vbs = [], [], [], [], [], [], [], []
    for l in range(DEPTH):
        xp = xp + 0.5 * swiglu(rms_norm(xp, norm_ffn1[l]), w_ffn1_in[l], w_ffn1_out[l])
        xs = xs + 0.5 * swiglu(rms_norm(xs, norm_ffn1[l]), w_ffn1_in[l], w_ffn1_out[l])

        qa, ka, va, qb, kb, vb, ga, gb = project(rms_norm(xp, norm_mix[l]), w_mix_in[l], b_gate[l], pos_p)
        ya = band_prompt(qa, ka, va, N_PAST_A, sinks[l], None)
        yb = band_prompt(qb, kb, vb, N_PAST_B, None, rel_bias[l])
        xp = xp + merge(ya, yb, ga, gb, w_branch_a[l], w_branch_b[l], w_mix_out[l])
        kwp.append(ka[:, s_len - keep_a:])
        vwp.append(va[:, s_len - keep_a:])
        kbp.append(kb[:, s_len - keep_b:])
        vbp.append(vb[:, s_len - keep_b:])

        qa, ka, va, qb, kb, vb, ga, gb = project(rms_norm(xs, norm_mix[l]), w_mix_in[l], b_gate[l], pos_s)
        ka_all = jnp.concatenate([cache_k_win[l], ka], axis=1)
        va_all = jnp.concatenate([cache_v_win[l], va], axis=1)
        kb_all = jnp.concatenate([cache_k_band[l], kb], axis=1)
        vb_all = jnp.concatenate([cache_v_band[l], vb], axis=1)
        ya = attend(qa, ka_all, va_all, pos_s, kpos_a, None, sinks[l], None)
        yb = attend(qb, kb_all, vb_all, pos_s, kpos_b, None, None, rel_bias[l])
        xs = xs + merge(ya, yb, ga, gb, w_branch_a[l], w_branch_b[l], w_mix_out[l])
        kws.append(ka_all[:, t_len:])
        vws.append(va_all[:, t_len:])
        kbs.append(kb_all[:, t_len:])
        vbs.append(vb_all[:, t_len:])

        xp = xp + 0.5 * swiglu(rms_norm(xp, norm_ffn2[l]), w_ffn2_in[l], w_ffn2_out[l])
        xs = xs + 0.5 * swiglu(rms_norm(xs, norm_ffn2[l]), w_ffn2_in[l], w_ffn2_out[l])

    y_prompt = rms_norm(xp, norm_final)
    y_sample = rms_norm(xs, norm_final)
    return (y_prompt, y_sample,
            jnp.stack(kwp), jnp.stack(vwp), jnp.stack(kbp), jnp.stack(vbp),
            jnp.stack(kws), jnp.stack(vws), jnp.stack(kbs), jnp.stack(vbs))
```

```python
import math
import numpy as np
import concourse.bass as bass
import concourse.mybir as mybir
from concourse.bass_utils import run_bass_kernel_spmd

F32 = mybir.dt.float32
BF16 = mybir.dt.bfloat16
I32 = mybir.dt.int32
ALU = mybir.AluOpType
AF = mybir.ActivationFunctionType

D = 1024
DFF = 2816
NF = 22
SEQ = 8192
OWN = 4096
HALO = 1024
SEG = OWN + HALO
TN = 512
NPT = SEG // TN
NS = 4
SN = NS * 64
PAST = 4096
NBLK = 188
SLOT = 4
NSLOT_L = NBLK // SLOT
NRING = 5
EPS = 1e-5
NEG = -30000.0
THETA = 500000.0
TWO_PI = 2.0 * math.pi

C_NORM = 0
C_BG = 56
C_SINK = 88
C_BC = 96
C_RI = 112
C_RM = 113
C_RS = 114
C_RO = 115
NCOLS = 116


class Buf:
    __slots__ = ("name", "w", "r", "track", "excl")

    def __init__(self, name, track=True, excl=False):
        self.name = name
        self.w = None
        self.r = []
        self.track = track
        self.excl = excl


class Entry:
    __slots__ = ("eng", "fn", "deps", "marked", "ev", "is_dma", "chan", "idx", "key", "tag")


class Chan:
    def __init__(self, nc, name):
        self.sem = nc.alloc_semaphore(name)
        self.cnt = 0
        self.key = name


class Prog:
    def __init__(self, nc):
        self.nc = nc
        self.entries = []
        self.engs = {"pe": nc.tensor, "act": nc.scalar, "dve": nc.vector, "pool": nc.gpsimd, "sp": nc.sync}
        self.sems = {k: nc.alloc_semaphore("sem_" + k) for k in self.engs}
        self.nchan = 0
        self.chans = []
        self.tag = ""

    def chan(self, name):
        c = Chan(self.nc, "ch%d_%s" % (self.nchan, name))
        self.nchan += 1
        self.chans.append(c)
        return c

    def op(self, eng, fn, reads=(), writes=(), chan=None, nowaw=False):
        e = Entry()
        e.eng = eng
        e.fn = fn
        e.is_dma = chan is not None
        e.marked = False
        e.chan = chan
        e.ev = None
        e.idx = len(self.entries)
        e.tag = self.tag
        e.key = chan.key if chan is not None else eng
        deps = {}

        def add(d):
            o = deps.get(d.key)
            if o is None or o.idx < d.idx:
                deps[d.key] = d
        for b in reads:
            if b.w is not None:
                add(b.w)
            if b.excl:
                for x in b.r:
                    if x.key != e.key:
                        add(x)
        for b in writes:
            if b.w is not None and not nowaw:
                add(b.w)
            for x in b.r:
                add(x)
        dl = []
        for d in deps.values():
            if d.eng == "pe" and eng == "pe" and not d.is_dma and chan is None:
                continue
            dl.append(d)
        e.deps = dl
        if chan is not None:
            chan.cnt += 1
            e.ev = (chan.sem, 16 * chan.cnt, chan.key)
        for b in reads:
            if b.track:
                b.r = [x for x in b.r if x.key != e.key]
                b.r.append(e)
        for b in writes:
            b.w = e
            b.r = []
        self.entries.append(e)
        return e

    def emit(self):
        for e in self.entries:
            for d in e.deps:
                if not d.is_dma:
                    d.marked = True
        counts = {k: 0 for k in self.engs}
        for e in self.entries:
            if not e.is_dma and e.marked:
                counts[e.eng] += 1
                e.ev = (self.sems[e.eng], counts[e.eng], e.eng)
        for k, v in counts.items():
            assert v < 60000, (k, v)
        waited = {k: {} for k in self.engs}
        nwait = 0
        for e in self.entries:
            h = self.engs[e.eng]
            need = {}
            for d in e.deps:
                sem, val, key = d.ev
                if need.get(key, (None, 0))[1] < val:
                    need[key] = (sem, val)
            w = waited[e.eng]
            for key, (sem, val) in need.items():
                if w.get(key, 0) < val:
                    h.wait_ge(sem, val)
                    w[key] = val
                    nwait += 1
            inst = e.fn()
            if e.is_dma:
                inst.then_inc(e.ev[0], 16)
            elif e.marked:
                inst.then_inc(e.ev[0], 1)
        sp = self.engs["sp"]
        for c in self.chans:
            if c.cnt > 0:
                sp.wait_ge(c.sem, 16 * c.cnt)
        return counts, nwait


def mk_ap(t, offset, ap):
    return bass.AP(tensor=t, offset=offset, ap=[list(x) for x in ap])


def build(n_ptiles=NPT, with_sample=True, n_layers=2, stop_after=None, halo_skip=True):
    nc = bass.Bass("TRN2", target_bir_lowering=False)
    P = Prog(nc)

    def dram_in(name, shape, dt=F32):
        return nc.dram_tensor(name, list(shape), dt, kind="ExternalInput")

    def dram_out(name, shape):
        return nc.dram_tensor(name, list(shape), F32, kind="ExternalOutput")

    xseg = dram_in("xseg", [SEG, D])
    xsmp = dram_in("xsmp", [SN, D])
    posd = dram_in("pos", [1, SEG + SN])
    vldd = dram_in("vld", [SEG, 64])
    wst = dram_in("wst", [2 * NSLOT_L, 128, SLOT * 1024])
    smallp = dram_in("smallp", [128, NCOLS])
    relx = dram_in("relx", [1, 16 * 385])
    identd = dram_in("ident", [128, 128])
    ckw = dram_in("ckw", [2, NS, 128, 128])
    cvw = dram_in("cvw", [2, NS, 128, 128])
    ckb = dram_in("ckb", [2, NS, 512, 512])
    cvb = dram_in("cvb", [2, NS, 512, 512])

    y_o = dram_out("y", [OWN, D])
    ys_o = dram_out("ys", [SN, D])
    kwp_o = dram_out("kwp", [2, 128, 128])
    vwp_o = dram_out("vwp", [2, 128, 128])
    kbp_o = dram_out("kbp", [2, 512, 512])
    vbp_o = dram_out("vbp", [2, 512, 512])
    kws_o = dram_out("kws", [2, NS, 128, 128])
    vws_o = dram_out("vws", [2, NS, 128, 128])
    kbs_o = dram_out("kbs", [2, NS, 512, 512])
    vbs_o = dram_out("vbs", [2, NS, 512, 512])

    wbf = nc.dram_tensor("wbf", [2 * NSLOT_L, 128, SLOT * 1024], BF16)
    rrep = nc.dram_tensor("rrep", [128, 16 * 385], F32)

    def sb(name, shape, dt):
        return nc.alloc_sbuf_tensor(name, list(shape), dt)

    xT = sb("xT", [128, 8, TN], F32)
    xn = sb("xn", [128, 8, TN], BF16)
    arenaG = sb("arenaG", [128, 22 * TN], BF16)
    arenaU = sb("arenaU", [128, 8 * TN], BF16)
    wring = sb("wring", [128, NRING, SLOT * 1024], BF16)
    kbT = [sb("kbT%d" % l, [128, 4, 1024], BF16) for l in range(2)]
    kaT = [sb("kaT%d" % l, [128, 1024], BF16) for l in range(2)]
    vbr = [sb("vbr%d" % l, [128, 8, 512], BF16) for l in range(2)]
    var_ = [sb("var%d" % l, [128, 8, 128], BF16) for l in range(2)]
    vldr = sb("vldr", [128, 8, 64], BF16)
    ones64 = sb("ones64", [128, 64], BF16)
    onesb = sb("onesb", [128, 128], BF16)
    ident = sb("ident_sb", [128, 128], F32)
    identb = sb("identb", [128, 128], BF16)
    smp = sb("smp", [128, NCOLS], F32)
    esink = sb("esink", [128, 8], F32)
    freq = sb("freq", [128, 1], F32)
    tb3 = [sb("tb3_%d" % l, [128, 8, 3, 128], BF16) for l in range(2)]
    mask_lo = sb("mask_lo", [128, 128], F32)
    mask_hi = sb("mask_hi", [128, 128], F32)
    cosT = sb("cosT", [128, TN], F32)
    sinT = sb("sinT", [128, TN], F32)
    posb = sb("posb", [128, TN], F32)
    NTMP = 6
    tmp32 = [sb("tmp32_%d" % i, [128, TN], F32) for i in range(NTMP)]
    tmpi = sb("tmpi", [128, TN], I32)
    rstd_sb = sb("rstd_sb", [128, TN], F32)
    B_rstd = Buf("rstd")
    pTb = [sb("pTb%d" % i, [128, 5, 128], BF16) for i in range(4)]
    pTa = [sb("pTa%d" % i, [128, 2, 4, 128], BF16) for i in range(3)]
    kaout = sb("kaout", [128, 4, 128], F32)
    vaout = sb("vaout", [128, 4, 128], F32)
    kcs_a = sb("kcs_a", [128, 128], F32)

    ps = nc.alloc_psum_tensor("ps", [128, 8 * 512], F32)

    B_xT = [Buf("xT%d" % c) for c in range(8)]
    B_xn = [Buf("xn%d" % c) for c in range(8)]
    stats = {"ready": None}
    B_G = {"cur": Buf("arenaG")}
    B_U = {"cur": Buf("arenaU")}
    B_ring = [Buf("ring%d" % i) for i in range(NRING)]
    B_kbT = [Buf("kbT%d" % l) for l in range(2)]
    B_kaT = [Buf("kaT%d" % l) for l in range(2)]
    B_vbr = [Buf("vbr%d" % l) for l in range(2)]
    B_var = [Buf("var%d" % l) for l in range(2)]
    B_vld = Buf("vldr")
    B_smp = Buf("smp", track=False)
    B_ident = Buf("ident", track=False)
    B_ones = Buf("ones", track=False)
    B_masks = Buf("masks", track=False)
    B_esink = Buf("esink", track=False)
    B_freq = Buf("freq", track=False)
    B_t34 = Buf("t34", track=False)
    B_rope = Buf("rope")
    B_posb = Buf("posb")
    B_tmp = [Buf("tmp%d" % i) for i in range(NTMP)]
    B_tmpi = Buf("tmpi")
    B_pTb = [Buf("pTb%d" % i) for i in range(4)]
    B_pTa = [Buf("pTa%d" % i) for i in range(3)]
    B_kaout = Buf("kaout")
    B_vaout = Buf("vaout")
    B_kcsa = Buf("kcsa")
    B_ps = [Buf("ps%d" % i, excl=True) for i in range(8)]
    B_wbf = [Buf("wbf%d" % i) for i in range(16)]
    B_rrep = Buf("rrep")
    B_dummy_out = Buf("dram_out")

    st = {"ps": 0, "tmp": 0, "ptb": 0, "pta": 0}

    def psum():
        i = st["ps"] % 4
        st["ps"] += 1
        return ps[:, i * 512:(i + 1) * 512], B_ps[i]

    def psum_fixed(i):
        return ps[:, i * 512:(i + 1) * 512], B_ps[i]

    def tmpf():
        i = st["tmp"] % NTMP
        st["tmp"] += 1
        return tmp32[i], B_tmp[i]

    def mm(out, lhsT, rhs, start, stop, reads, writes):
        P.op("pe", lambda: nc.tensor.matmul(out, lhsT=lhsT, rhs=rhs, start=start, stop=stop), reads, writes)

    def tr(out, in_, idn, reads, writes):
        P.op("pe", lambda: nc.tensor.transpose(out, in_, idn), reads, writes)

    def act(out, in_, func, reads, writes, bias=None, scale=None):
        kw = {}
        if bias is not None:
            kw["bias"] = bias
        if scale is not None:
            kw["scale"] = scale
        P.op("act", lambda: nc.scalar.activation(out=out, in_=in_, func=func, **kw), reads, writes)

    def tt(eng, out, in0, in1, op, reads, writes):
        h = nc.vector if eng == "dve" else nc.gpsimd
        P.op(eng, lambda: h.tensor_tensor(out=out, in0=in0, in1=in1, op=op), reads, writes)

    def ts(eng, out, in0, s1, s2, op0, op1, reads, writes):
        h = nc.vector if eng == "dve" else nc.gpsimd
        if op1 is None:
            P.op(eng, lambda: h.tensor_scalar(out=out, in0=in0, scalar1=s1, scalar2=None, op0=op0), reads, writes)
        else:
            P.op(eng, lambda: h.tensor_scalar(out=out, in0=in0, scalar1=s1, scalar2=s2, op0=op0, op1=op1), reads, writes)

    def stt(out, in0, scalar, in1, op0, op1, reads, writes):
        P.op("dve", lambda: nc.vector.scalar_tensor_tensor(out=out, in0=in0, scalar=scalar, in1=in1, op0=op0, op1=op1),
             reads, writes)

    def cp(eng, out, in_, reads, writes):
        if eng == "act":
            P.op("act", lambda: nc.scalar.copy(out=out, in_=in_), reads, writes)
        else:
            h = nc.vector if eng == "dve" else nc.gpsimd
            P.op(eng, lambda: h.tensor_copy(out=out, in_=in_), reads, writes)

    def recip(out, in_, reads, writes):
        P.op("dve", lambda: nc.vector.reciprocal(out=out, in_=in_), reads, writes)

    def memset(eng, ap, val, writes):
        h = nc.vector if eng == "dve" else nc.gpsimd
        P.op(eng, lambda: h.memset(ap, val), (), writes)

    def dma(q, out, in_, chan, reads, writes, nowaw=False):
        h = nc.sync if q == "sp" else nc.gpsimd
        P.op(q, lambda: h.dma_start(out=out, in_=in_), reads, writes, chan=chan, nowaw=nowaw)

    def retarget(box, name):
        old = box["cur"]
        nb = Buf(name)
        nb.r = list(old.r) + ([old.w] if old.w is not None else [])
        box["cur"] = nb
        return nb

    def gview_f32(off_bytes, shape):
        v = arenaG[:, off_bytes // 2: off_bytes // 2 + 2 * int(np.prod(shape[1:]))].bitcast(F32)
        return v

    ch_c = P.chan("t34")
    ch_smp = P.chan("smp")
    ch_id = P.chan("ident")
    dma("sp", smp[:], smallp.ap(), ch_smp, (), (B_smp,))
    dma("sp", ident[:], identd.ap(), ch_id, (), (B_ident,))
    cp("dve", identb[:], ident[:], (B_ident,), (B_ones,))
    memset("pool", onesb[:], 1.0, (B_ones,))
    memset("pool", ones64[:], 1.0, (B_ones,))
    memset("pool", mask_lo[:], 0.0, (B_masks,))
    memset("pool", mask_hi[:], 0.0, (B_masks,))
    memset("pool", mask_lo[0:64, 64:128], NEG, (B_masks,))
    memset("pool", mask_hi[64:128, 0:64], NEG, (B_masks,))
    for l in range(2):
        memset("pool", kbT[l][:], 0.0, (B_kbT[l],))
        memset("pool", kaT[l][:], 0.0, (B_kaT[l],))
        memset("pool", vbr[l][:], 0.0, (B_vbr[l],))
        memset("pool", var_[l][:], 0.0, (B_var[l],))
    memset("pool", vldr[:], 0.0, (B_vld,))
    act(esink[:], smp[:, C_SINK:C_SINK + 8], AF.Exp, (B_smp,), (B_esink,))
    act(freq[:], smp[:, C_RI:C_RI + 1], AF.Exp, (B_smp,), (B_freq,), scale=-math.log(THETA) / 8.0)
    ch_r = P.chan("rrep")
    dma("sp", rrep.ap(), mk_ap(relx, 0, [[0, 128], [1, 16 * 385]]), ch_r, (), (B_rrep,))
    L_ = 385
    tzs = sb("tzs", [128, 1024], F32)
    B_tzs = Buf("tzs")
    ch_tz = P.chan("tz")

    def toep_load(cc, j0, npart, p0, conv):
        src = mk_ap(rrep, (L_ - 1 - cc + j0) * 16, [[16 * L_ - 16, npart], [1, 1024]])
        dma("sp", tzs[p0:p0 + npart, :], src, ch_tz, (B_rrep,), (B_tzs,))
        for l in range(2):
            sv = mk_ap(tzs, p0 * 1024 + l * 8, [[1024, npart], [1, 8], [16, 64]])
            ts("dve", conv(l), sv, 8.0, None, ALU.mult, None, (B_tzs, B_t34), (B_t34,))

    for bi, cc in ((1, 128), (2, 256)):
        for jh in range(2):
            toep_load(cc, jh * 64, 128, 0, lambda l, bi=bi, jh=jh: tb3[l][:, :, bi, jh * 64:(jh + 1) * 64])
    for l in range(2):
        ts("dve", tb3[l][64:128, :, 2, 0:64], tb3[l][64:128, :, 2, 0:64], 8.0 * NEG, None, ALU.add, None, (B_t34,), (B_t34,))
        for h in range(8):
            ts("dve", tb3[l][:, h, 0, :], mask_lo[:, :], smp[:, C_BC + l * 8 + h:C_BC + l * 8 + h + 1], 8.0, ALU.add,
               ALU.mult, (B_masks, B_smp, B_t34), (B_t34,))

    gsz = [1, 1, 2, 4] + [8] * 11
    grp_of = []
    for gi_, n_ in enumerate(gsz):
        grp_of.extend([gi_] * n_)
    grp_of = grp_of[:2 * NSLOT_L]
    ch_pc = [P.chan("pc%d" % i) for i in range(len(gsz))]
    for s in range(2 * NSLOT_L):
        gi = grp_of[s]
        dma("pool", wbf[s], wst[s], ch_pc[gi], (), (B_wbf[gi],), nowaw=True)

    ch_cc = P.chan("cachecopy")
    if with_sample:
        for l in range(2):
            dma("pool", kws_o[l, :, 0:64, :], ckw[l, :, 64:128, :], ch_cc, (), ())
            dma("pool", vws_o[l, :, 0:64, :], cvw[l, :, 64:128, :], ch_cc, (), ())
            for s in range(NS):
                dma("pool", kbs_o[l, s, 0:448, :], ckb[l, s, 64:512, :], ch_cc, (), ())
                dma("pool", vbs_o[l, s, 0:448, :], cvb[l, s, 64:512, :], ch_cc, (), ())

    ch_ring = [P.chan("ring%d" % i) for i in range(NRING)]
    ws = {"issued": 0, "pos": 0, "cur": 0, "lbase": 0}

    def tile_modes(kind, t):
        if kind == "sample" or not halo_skip:
            return ["FULL"] * n_layers
        if t == 0:
            return (["KV", "NONE"])[:n_layers]
        if t == 1:
            return (["FULL", "KV"])[:n_layers]
        return ["FULL"] * n_layers

    def layer_slots(mode):
        if mode == "FULL":
            return list(range(NSLOT_L))
        if mode == "KV":
            return list(range(17)) + [18, 19, 20, 21, 22]
        return []

    sched = []
    for t_ in range(n_ptiles):
        for l_, m_ in enumerate(tile_modes("prompt", t_)):
            sched.extend(l_ * NSLOT_L + x for x in layer_slots(m_))
    if with_sample:
        for l_, m_ in enumerate(tile_modes("sample", None)):
            sched.extend(l_ * NSLOT_L + x for x in layer_slots(m_))

    def issue_slot():
        g = ws["issued"]
        if g >= len(sched):
            return
        s_ = sched[g]
        r = g % NRING
        dma("sp", wring[:, r, :], wbf[s_], ch_ring[r], (B_wbf[grp_of[s_]],), (B_ring[r],))
        ws["issued"] += 1

    for _ in range(NRING - 1):
        issue_slot()

    def next_blocks(n):
        outl = []
        for _ in range(n):
            b = ws["pos"]
            dslot = ws["lbase"] + b // SLOT
            if sched[ws["cur"]] != dslot:
                ws["cur"] += 1
            assert sched[ws["cur"]] == dslot, (ws["cur"], sched[ws["cur"]], dslot)
            while ws["issued"] <= ws["cur"] + NRING - 2 and ws["issued"] < len(sched):
                issue_slot()
            r = ws["cur"] % NRING
            k = b % SLOT
            outl.append((wring[:, r, k * 1024:(k + 1) * 1024], B_ring[r]))
            ws["pos"] += 1
        return outl

    def skip_blocks(n):
        ws["pos"] += n

    def stats_chunk(c, N, pst, Bpst):
        t, Bt = tmpf()
        tb16 = t[:, :].bitcast(BF16)
        act(tb16[:, 0:N], xT[:, c, 0:N], AF.Square, (B_xT[c],), (Bt,))
        mm(pst[:, 0:N], onesb[:], tb16[:, 0:N], c == 0, c == 7, (Bt, B_ones), (Bpst,))

    def get_rstd(N):
        if stats["ready"] is None:
            pst, Bpst = psum_fixed(6)
            for c in range(8):
                stats_chunk(c, N, pst, Bpst)
        else:
            pst, Bpst = stats["ready"]
            stats["ready"] = None
        t1, Bt1 = tmpf()
        act(t1[:, 0:N], pst[:, 0:N], AF.Sqrt, (Bpst,), (Bt1,), bias=EPS, scale=1.0 / D)
        recip(rstd_sb[:, 0:N], t1[:, 0:N], (Bt1,), (B_rstd,))

    def rmsnorm(N, gcol):
        P.tag = "norm"
        get_rstd(N)
        for c in range(8):
            stt(xn[:, c, 0:N], xT[:, c, 0:N], smp[:, gcol + c:gcol + c + 1], rstd_sb[:, 0:N], ALU.mult, ALU.mult,
                (B_xT[c], B_rstd, B_smp), (B_xn[c],))

    def ffn(N, gcol):
        rmsnorm(N, gcol)
        Bg = retarget(B_G, "g")
        P.tag = "ffn_in"
        g = arenaG[:, 0:NF * TN].rearrange("p (f t) -> p f t", f=NF)
        for i in range(NF):
            (wa, Bwa), (wb, Bwb) = next_blocks(2)
            pa, Bpa = psum()
            pb, Bpb = psum()
            for kc in range(8):
                mm(pa[:, 0:N], wa[:, kc * 128:(kc + 1) * 128], xn[:, kc, 0:N], kc == 0, kc == 7, (Bwa, B_xn[kc]), (Bpa,))
            for kc in range(8):
                mm(pb[:, 0:N], wb[:, kc * 128:(kc + 1) * 128], xn[:, kc, 0:N], kc == 0, kc == 7, (Bwb, B_xn[kc]), (Bpb,))
            t, Bt = tmpf()
            act(t[:, 0:N], pa[:, 0:N], AF.Silu, (Bpa,), (Bt,))
            tt("dve", g[:, i, 0:N], t[:, 0:N], pb[:, 0:N], ALU.mult, (Bt, Bpb), (Bg,))
        pos_ = []
        P.tag = "ffn_out"
        for dj in range(8):
            pos_.append(psum_fixed(dj))
        TAIL = 4
        for i in range(NF - TAIL):
            (wo, Bwo), = next_blocks(1)
            for dj in range(8):
                po, Bpo = pos_[dj]
                mm(po[:, 0:N], wo[:, dj * 128:(dj + 1) * 128], g[:, i, 0:N], i == 0, False, (Bwo, Bg), (Bpo,))
        tail = next_blocks(TAIL)
        for dj in range(8):
            po, Bpo = pos_[dj]
            for k_, (wo, Bwo) in enumerate(tail):
                i = NF - TAIL + k_
                mm(po[:, 0:N], wo[:, dj * 128:(dj + 1) * 128], g[:, i, 0:N], False, i == NF - 1, (Bwo, Bg), (Bpo,))
        P.tag = "ffn_evac"
        pst, Bpst = psum_fixed(0)
        for dj in range(8):
            po, Bpo = pos_[dj]
            stt(xT[:, dj, 0:N], po[:, 0:N], 0.5, xT[:, dj, 0:N], ALU.mult, ALU.add, (Bpo, B_xT[dj]), (B_xT[dj],))
            stats_chunk(dj, N, pst, Bpst)
        stats["ready"] = (pst, Bpst)

    def rope_tables(N, pos_off):
        ch = ch_pos
        dma("sp", posb[:, 0:N], mk_ap(posd, pos_off, [[0, 128], [1, N]]), ch, (), (B_posb,))
        ang, Ba = tmpf()
        ts("dve", ang[:, 0:N], posb[:, 0:N], freq[:, 0:1], None, ALU.mult, None, (B_posb, B_freq), (Ba,))
        for which in range(2):
            src, Bs = ang, Ba
            if which == 1:
                a2, Ba2 = tmpf()
                ts("dve", a2[:, 0:N], ang[:, 0:N], math.pi / 2.0, None, ALU.add, None, (Ba,), (Ba2,))
                src, Bs = a2, Ba2
            ts("dve", tmpi[:, 0:N], src[:, 0:N], 1.0 / TWO_PI, None, ALU.mult, None, (Bs,), (B_tmpi,))
            kf, Bk = tmpf()
            cp("dve", kf[:, 0:N], tmpi[:, 0:N], (B_tmpi,), (Bk,))
            r, Br = tmpf()
            stt(r[:, 0:N], kf[:, 0:N], -TWO_PI, src[:, 0:N], ALU.mult, ALU.add, (Bk, Bs), (Br,))
            ts("dve", r[:, 0:N], r[:, 0:N], -math.pi, math.pi, ALU.max, ALU.min, (Br,), (Br,))
            dst = sinT if which == 0 else cosT
            act(dst[:, 0:N], r[:, 0:N], AF.Sin, (Br,), (B_rope,))
        ts("dve", cosT[:, 0:N], cosT[:, 0:N], smp[:, C_RM:C_RM + 1], smp[:, C_RO:C_RO + 1], ALU.mult, ALU.add,
           (B_rope, B_smp), (B_rope,))
        ts("dve", sinT[:, 0:N], sinT[:, 0:N], smp[:, C_RS:C_RS + 1], None, ALU.mult, None, (B_rope, B_smp), (B_rope,))

    def proj_fm(N, w, Bw):
        pz, Bz = psum()
        for kc in range(8):
            mm(pz[:, 0:N], w[:, kc * 128:(kc + 1) * 128], xn[:, kc, 0:N], kc == 0, kc == 7, (Bw, B_xn[kc]), (Bz,))
        return pz, Bz

    def rope_block(N, dst_ap, Bdst, f32_dst=None, Bf32=None):
        (w, Bw), (w2, Bw2) = next_blocks(2)
        pz, Bz = proj_fm(N, w, Bw)
        pz2, Bz2 = proj_fm(N, w2, Bw2)
        t1, Bt1 = tmpf()
        t2, Bt2 = tmpf()
        tt("dve", t1[:, 0:N], pz[:, 0:N], cosT[:, 0:N], ALU.mult, (Bz, B_rope), (Bt1,))
        tt("dve", t2[:, 0:N], pz2[:, 0:N], sinT[:, 0:N], ALU.mult, (Bz2, B_rope), (Bt2,))
        if f32_dst is None:
            tt("pool", dst_ap, t1[:, 0:N], t2[:, 0:N], ALU.add, (Bt1, Bt2), (Bdst,))
        else:
            tt("pool", f32_dst[:, 0:N], t1[:, 0:N], t2[:, 0:N], ALU.add, (Bt1, Bt2), (Bf32,))
            cp("act", dst_ap, f32_dst[:, 0:N], (Bf32,), (Bdst,))

    def transpose_out(N, src_f32, Bsrc, dst_stage, Bdst, col0, ncols=128, rows=128):
        nb = N // 128
        pt, Bp = psum()
        for tb in range(nb):
            tr(pt[:, tb * 128: tb * 128 + rows], src_f32[0:rows, tb * 128:(tb + 1) * 128], ident[0:rows, 0:rows],
               (Bsrc, B_ident), (Bp,))
        cp("dve", dst_stage[:, 0:nb, col0:col0 + rows],
           pt[:, 0:nb * 128].rearrange("p (b c) -> p b c", b=nb)[:, :, 0:rows], (Bp,), (Bdst,))

    def mixer(N, l, tinfo):
        kind = tinfo["kind"]
        is_out = tinfo["out"] and not _CACHE.get("no_out", False)
        nb = N // 128
        rmsnorm(N, C_NORM + (l * 3 + 1) * 8)
        retarget(B_G, "mix")
        Bq = Buf("qy")
        Bq.r = list(B_G["cur"].r)
        Bqa, Bqb, Bya, Byb = Buf("qa"), Buf("qb"), Buf("ya"), Buf("yb")
        for b_ in (Bqa, Bqb, Bya, Byb):
            b_.r = list(B_G["cur"].r)
        qaT = arenaG[:, 0 * TN:4 * TN].rearrange("p (j t) -> p j t", j=4)
        qbT = arenaG[:, 4 * TN:8 * TN].rearrange("p (j t) -> p j t", j=4)
        yaT = arenaG[:, 8 * TN:12 * TN].rearrange("p (j t) -> p j t", j=4)
        ybT = arenaG[:, 12 * TN:16 * TN].rearrange("p (j t) -> p j t", j=4)
        Bstg = Buf("stg")
        Bstg.r = list(B_G["cur"].r)
        if kind == "prompt":
            t = tinfo["t"]
            rh = (t % 2) * 512
            rb = (t % 2) * 4
            kb_dst = lambda j: kbT[l][:, j, rh:rh + N]
            ka_dst = kaT[l][:, rh:rh + N]
            Bkb, Bka, Bvb, Bva = B_kbT[l], B_kaT[l], B_vbr[l], B_var[l]
            vb_dst = lambda tb: vbr[l][:, rb + tb, :]
            va_dst = var_[l][:, rb:rb + nb, :]
        else:
            kb_dst = lambda j: knewb[:, j, 0:N]
            ka_dst = knewa[:, 0:N]
            Bkb, Bka, Bvb, Bva = B_knb, B_kna, B_vnb, B_vna
            vb_dst = lambda tb: vnewb[:, tb, :]
            va_dst = vnewa[:, 0:nb, :]

        kv_only = (tinfo.get("mode") == "KV")
        P.tag = "mix_proj"
        if kv_only:
            skip_blocks(8)
        else:
            for j in range(4):
                rope_block(N, qaT[:, j, 0:N], Bqa)
        kaf, Bkaf = tmpf()
        rope_block(N, ka_dst, Bka, f32_dst=kaf, Bf32=Bkaf)
        if is_out:
            transpose_out(N, kaf, Bkaf, kaout, B_kaout, 0)
        wv = next_blocks(4)
        wv_ap = wv[0][0]
        Bwv = wv[0][1]
        r_ = ws["cur"] % NRING
        wv3 = wring[:, r_, :].rearrange("p (k c) -> p k c", k=8)
        if is_out:
            Bvout = retarget(B_U, "vout")
            vout = arenaU[:, :].bitcast(F32).rearrange("p (b c) -> p b c", b=4)
        for tb in range(nb):
            pv, Bpv = psum()
            for kc in range(8):
                mm(pv[:, 0:512], xn[:, kc, tb * 128:(tb + 1) * 128], wv3[:, kc, :], kc == 0, kc == 7, (Bwv, B_xn[kc]), (Bpv,))
            cp("act", vb_dst(tb), pv[:, 0:512], (Bpv,), (Bvb,))
            if is_out:
                cp("dve", vout[:, tb, :], pv[:, 0:512], (Bpv,), (Bvout,))
        if is_out:
            store_v_band(l, tinfo, vout, Bvout, nb)
        (wva, Bwva), = next_blocks(1)
        pv, Bpv = psum()
        for tb in range(nb):
            for kc in range(8):
                mm(pv[:, tb * 128:(tb + 1) * 128], xn[:, kc, tb * 128:(tb + 1) * 128], wva[:, kc * 128:(kc + 1) * 128],
                   kc == 0, kc == 7, (Bwva, B_xn[kc]), (Bpv,))
        pv3 = pv[:, 0:nb * 128].rearrange("p (b c) -> p b c", b=nb)
        cp("act", va_dst, pv3, (Bpv,), (Bva,))
        if is_out:
            cp("dve", vaout[:, 0:nb, :], pv3, (Bpv,), (B_vaout,))
        for j in range(4):
            if kv_only:
                skip_blocks(1)
                continue
            (w, Bw), = next_blocks(1)
            pz, Bz = proj_fm(N, w, Bw)
            cp("act", qbT[:, j, 0:N], pz[:, 0:N], (Bz,), (Bqb,))
        if is_out:
            Bkout = retarget(B_U, "kout")
            kout = arenaU[:, :].bitcast(F32).rearrange("p (b c) -> p b c", b=4)
        for j in range(4):
            (w, Bw), = next_blocks(1)
            pz, Bz = proj_fm(N, w, Bw)
            cp("act", kb_dst(j), pz[:, 0:N], (Bz,), (Bkb,))
            if is_out:
                kf, Bkf = tmpf()
                cp("dve", kf[:, 0:N], pz[:, 0:N], (Bz,), (Bkf,))
                transpose_out(N, kf, Bkf, kout, Bkout, j * 128)
        if is_out:
            store_k_band(l, tinfo, kout, Bkout, nb)
            store_win(l, tinfo, nb)


        if kv_only:
            skip_blocks(32)
            cur = B_G["cur"]
            for b_ in (Bqa, Bqb, Bya, Byb, Bstg):
                cur.r.extend(b_.r)
                if b_.w is not None:
                    cur.r.append(b_.w)
            return
        if stop_after == "mixproj":
            return
        all_units = []
        if kind == "prompt":
            t = tinfo["t"]
            for p in range(nb):
                gB = t * 4 + p
                keysB = []
                for i_, kb_ in enumerate(range(gB - 4, gB + 1)):
                    rbk = kb_ % 8
                    keysB.append(dict(kT=lambda rows, hp, rbk=rbk: kbT[l][rows, hp, rbk * 128:(rbk + 1) * 128],
                                      v=lambda h, rbk=rbk: vbr[l][:, rbk, h * 64:(h + 1) * 64],
                                      vld=vldr[:, rbk, :], nk=128, base=0, kind=i_, Bk=B_kbT[l], Bv=B_vbr[l],
                                      Bvld=B_vld))
                keysA = []
                for i_, kb_ in enumerate(range(gB - 1, gB + 1)):
                    rbk = kb_ % 8
                    keysA.append(dict(kT=lambda rows, rbk=rbk: kaT[l][rows, rbk * 128:(rbk + 1) * 128],
                                      v=lambda kv, rbk=rbk: var_[l][:, rbk, kv * 64:(kv + 1) * 64],
                                      vld=vldr[:, rbk, :], nk=128, base=0, kind=i_, Bk=B_kaT[l], Bv=B_var[l],
                                      Bvld=B_vld))
                all_units.extend(attend(l, p * 128, 128, keysA, keysB, qaT, qbT, yaT, ybT, Bqa, Bqb, Bya, Byb))
        else:
            for s in range(NS):
                rs = s % 2
                load_cache(l, s, rs, Bstg)
                base = (s % 2) * 64
                tbk = s // 2
                keysB = []
                for i_ in range(4):
                    keysB.append(dict(kT=lambda rows, hp, i_=i_: kbT[rs][rows, hp, i_ * 128:(i_ + 1) * 128],
                                      v=lambda h, i_=i_: vbr[rs][:, i_, h * 64:(h + 1) * 64],
                                      vld=ones64[:, :], nk=128, base=0, kind=i_, Bk=B_kbT[rs], Bv=B_vbr[rs],
                                      Bvld=B_ones))
                keysB.append(dict(kT=lambda rows, hp: knewb[rows, hp, s * 64:(s + 1) * 64],
                                  v=lambda h: vnewb[base:base + 64, tbk, h * 64:(h + 1) * 64],
                                  vld=ones64[base:base + 64, :], nk=64, base=base, kind=4, Bk=B_knb, Bv=B_vnb,
                                  Bvld=B_ones))
                keysA = [dict(kT=lambda rows: kaT[rs][rows, 0:128],
                              v=lambda kv: var_[rs][:, 0, kv * 64:(kv + 1) * 64],
                              vld=ones64[:, :], nk=128, base=0, kind=0, Bk=B_kaT[rs], Bv=B_var[rs], Bvld=B_ones),
                         dict(kT=lambda rows: knewa[rows, s * 64:(s + 1) * 64],
                              v=lambda kv: vnewa[base:base + 64, tbk, kv * 64:(kv + 1) * 64],
                              vld=ones64[base:base + 64, :], nk=64, base=base, kind=1, Bk=B_kna, Bv=B_vna,
                              Bvld=B_ones)]
                run_units(attend(l, s * 64, 64, keysA, keysB, qaT, qbT, yaT, ybT, Bqa, Bqb, Bya, Byb))

        P.tag = "attn"
        run_units(all_units)
        P.tag = "merge"
        if stop_after == "attn":
            return
        Bu = retarget(B_U, "u")
        u = arenaU[:, 0:8 * TN].rearrange("p (c t) -> p c t", c=8)
        for j in range(8):
            (wga, Bwga), (wgb, Bwgb), (wbr, Bwbr) = next_blocks(3)
            pga, Bpga = proj_fm(N, wga, Bwga)
            pgb, Bpgb = proj_fm(N, wgb, Bwgb)
            ga, Bga = tmpf()
            gb, Bgb = tmpf()
            act(ga[:, 0:N], pga[:, 0:N], AF.Sigmoid, (Bpga, B_smp), (Bga,),
                bias=smp[:, C_BG + l * 16 + j:C_BG + l * 16 + j + 1])
            act(gb[:, 0:N], pgb[:, 0:N], AF.Sigmoid, (Bpgb, B_smp), (Bgb,),
                bias=smp[:, C_BG + l * 16 + 8 + j:C_BG + l * 16 + 8 + j + 1])
            pua, Bpua = psum()
            pub, Bpub = psum()
            for kc in range(4):
                mm(pua[:, 0:N], wbr[:, kc * 128:(kc + 1) * 128], yaT[:, kc, 0:N], kc == 0, kc == 3, (Bwbr, Bya), (Bpua,))
            for kc in range(4):
                mm(pub[:, 0:N], wbr[:, 512 + kc * 128:512 + (kc + 1) * 128], ybT[:, kc, 0:N], kc == 0, kc == 3,
                   (Bwbr, Byb), (Bpub,))
            tt("dve", ga[:, 0:N], pua[:, 0:N], ga[:, 0:N], ALU.mult, (Bpua, Bga), (Bga,))
            tt("dve", gb[:, 0:N], pub[:, 0:N], gb[:, 0:N], ALU.mult, (Bpub, Bgb), (Bgb,))
            tt("pool", u[:, j, 0:N], ga[:, 0:N], gb[:, 0:N], ALU.add, (Bga, Bgb), (Bu,))
        P.tag = "mix_out"
        pst, Bpst = psum_fixed(6)
        for dj in range(8):
            (w, Bw), = next_blocks(1)
            po, Bpo = psum()
            for kc in range(8):
                mm(po[:, 0:N], w[:, kc * 128:(kc + 1) * 128], u[:, kc, 0:N], kc == 0, kc == 7, (Bw, Bu), (Bpo,))
            tt("dve", xT[:, dj, 0:N], po[:, 0:N], xT[:, dj, 0:N], ALU.add, (Bpo, B_xT[dj]), (B_xT[dj],))
            if dj >= 1:
                stats_chunk(dj - 1, N, pst, Bpst)
        stats_chunk(7, N, pst, Bpst)
        stats["ready"] = (pst, Bpst)
        cur = B_G["cur"]
        for b_ in (Bqa, Bqb, Bya, Byb, Bstg):
            cur.r.extend(b_.r)
            if b_.w is not None:
                cur.r.append(b_.w)

    def attend(l, q0, nq, keysA, keysB, qaT, qbT, yaT, ybT, Bqa, Bqb, Bya, Byb):
        units = []
        fast = (nq == 128)
        pO, BpO = psum_fixed(4)
        pD, BpD = psum_fixed(5)
        nkb = len(keysB)
        pslot = {0: 0, 3: 1, 4: 2, 1: 3, 2: 4}
        slot = {0: (0, 0), 3: (0, 1), 4: (0, 2), 1: (1, 0), 2: (1, 1)}

        def make_b(h):
            hp, half = h // 2, h % 2
            rows = slice(half * 64, half * 64 + 64)
            stt_ = {}

            def qk():
                pS, BpS = psum()
                pS2, BpS2 = psum()
                pi = st["ptb"] % 4
                st["ptb"] += 1
                pT, BpT = pTb[pi], B_pTb[pi]
                stt_["pT"] = (pT, BpT)
                dsts = []
                if fast:
                    mm(pS[:, 0:384], identb[:, :], tb3[l][:, h, :, :].rearrange("p b c -> p (b c)"), True, False,
                       (B_ones, B_t34), (BpS,))
                for i_, kb_ in enumerate(keysB):
                    nk, base = kb_["nk"], kb_["base"]
                    bank, sl_ = slot[kb_["kind"]]
                    if bank == 0:
                        dst = pS[base:base + nk, sl_ * 128:sl_ * 128 + nq]
                        Bd = BpS
                    else:
                        dst = pS2[base:base + nk, sl_ * 128:sl_ * 128 + nq]
                        Bd = BpS2
                    if fast and bank == 0:
                        mm(dst, kb_["kT"](rows, hp), qbT[rows, hp, q0:q0 + nq], False, kb_["kind"] == 4,
                           (kb_["Bk"], Bqb), (Bd,))
                    else:
                        mm(dst, kb_["kT"](rows, hp), qbT[rows, hp, q0:q0 + nq], True, True, (kb_["Bk"], Bqb), (Bd,))
                    dsts.append((dst, Bd))
                bcol = smp[:, C_BC + l * 8 + h:C_BC + l * 8 + h + 1]
                if fast:
                    act(pT[:, 3:5, :], pS2[:, 0:256].rearrange("p (b c) -> p b c", b=2), AF.Exp, (BpS2, B_smp), (BpT,),
                        bias=bcol, scale=0.125)
                    act(pT[:, 0:3, :], pS[:, 0:384].rearrange("p (b c) -> p b c", b=3), AF.Exp, (BpS,), (BpT,),
                        scale=0.125)
                    return
                for i_, kb_ in enumerate(keysB):
                    nk, base = kb_["nk"], kb_["base"]
                    dst, Bd = dsts[i_]
                    pr = slice(base, base + nk)
                    kd = kb_["kind"]
                    ps_ = pslot[kd]
                    if kd in (1, 2):
                        act(pT[pr, ps_, 0:nq], dst, AF.Exp, (Bd, B_smp), (BpT,), bias=bcol[pr, :], scale=0.125)
                    else:
                        tm, Btm = tmpf()
                        bi = {0: 0, 3: 1, 4: 2}[kd]
                        btile = tb3[l][0:nk, h, bi, 0:nq] if base == 0 else t34_hi[l][pr, 0, h, 0:nq]
                        tt("dve", tm[pr, 0:nq], dst, btile, ALU.add, (Bd, B_t34), (Btm,))
                        act(pT[pr, ps_, 0:nq], tm[pr, 0:nq], AF.Exp, (Btm,), (BpT,), scale=0.125)

            def pv():
                pT, BpT = stt_["pT"]
                for i_, kb_ in enumerate(keysB):
                    nk, base = kb_["nk"], kb_["base"]
                    pr = slice(base, base + nk)
                    mm(pO[rows, hp * 128:hp * 128 + nq], kb_["v"](h), pT[pr, pslot[kb_["kind"]], 0:nq], i_ == 0,
                       i_ == nkb - 1, (kb_["Bv"], BpT), (BpO,))
                for i_, kb_ in enumerate(keysB):
                    nk, base = kb_["nk"], kb_["base"]
                    pr = slice(base, base + nk)
                    mm(pD[rows, hp * 128:hp * 128 + nq], kb_["vld"], pT[pr, pslot[kb_["kind"]], 0:nq], i_ == 0,
                       i_ == nkb - 1, (kb_["Bvld"], BpT), (BpD,))
            return qk, pv

        def post_b():
            pO3 = pO[:, :].rearrange("p (j c) -> p j c", j=4)[:, :, 0:nq]
            pD3 = pD[:, :].rearrange("p (j c) -> p j c", j=4)[:, :, 0:nq]
            rc, Brc = tmpf()
            rc3 = rc[:, :].rearrange("p (j c) -> p j c", j=4)[:, :, 0:nq]
            ts("dve", rc3, pD3, 1e-30, None, ALU.add, None, (BpD,), (Brc,))
            recip(rc3, rc3, (Brc,), (Brc,))
            tt("dve", ybT[:, :, q0:q0 + nq], pO3, rc3, ALU.mult, (BpO, Brc), (Byb,))

        for h in range(8):
            qk, pv = make_b(h)
            units.append((qk, pv, post_b if h == 7 else None))

        pOa, BpOa = psum_fixed(6)
        pDa, BpDa = psum_fixed(7)
        nka = len(keysA)

        def make_a(kv):
            rows = slice(kv * 64, kv * 64 + 64)
            stt_ = {}

            def qk():
                pi = st["pta"] % 3
                st["pta"] += 1
                pT, BpT = pTa[pi], B_pTa[pi]
                stt_["pT"] = (pT, BpT)
                dsts = []
                for i_, ka_ in enumerate(keysA):
                    nk, base = ka_["nk"], ka_["base"]
                    pS, BpS = psum()
                    pS3 = pS[:, 0:4 * nq].rearrange("p (j c) -> p j c", j=4)
                    dst = pS3[base:base + nk, :, :]
                    mm(dst, ka_["kT"](rows), qaT[rows, :, q0:q0 + nq], True, True, (ka_["Bk"], Bqa), (BpS,))
                    dsts.append((dst, BpS))
                for i_, ka_ in enumerate(keysA):
                    nk, base = ka_["nk"], ka_["base"]
                    pr = slice(base, base + nk)
                    dst, BpS = dsts[i_]
                    act(pT[pr, i_, :, 0:nq], dst, AF.Exp, (BpS,), (BpT,), scale=0.125)
                    if nq == 128:
                        if ka_["kind"] == 0:
                            memset("pool", pT[0:64, i_, :, 64:128], 0.0, (BpT,))
                        else:
                            memset("pool", pT[64:128, i_, :, 0:64], 0.0, (BpT,))

            def pv():
                pT, BpT = stt_["pT"]
                for i_, ka_ in enumerate(keysA):
                    nk, base = ka_["nk"], ka_["base"]
                    pr = slice(base, base + nk)
                    mm(pOa[rows, 0:4 * nq].rearrange("p (j c) -> p j c", j=4), ka_["v"](kv), pT[pr, i_, :, 0:nq],
                       i_ == 0, i_ == nka - 1, (ka_["Bv"], BpT), (BpOa,))
                for i_, ka_ in enumerate(keysA):
                    nk, base = ka_["nk"], ka_["base"]
                    pr = slice(base, base + nk)
                    mm(pDa[rows, 0:4 * nq].rearrange("p (j c) -> p j c", j=4), ka_["vld"], pT[pr, i_, :, 0:nq],
                       i_ == 0, i_ == nka - 1, (ka_["Bvld"], BpT), (BpDa,))
            return qk, pv

        def post_a():
            pO3 = pOa[:, 0:4 * nq].rearrange("p (j c) -> p j c", j=4)
            pD3 = pDa[:, 0:4 * nq].rearrange("p (j c) -> p j c", j=4)
            rc, Brc = tmpf()
            rc3 = rc[:, :].rearrange("p (j c) -> p j c", j=4)[:, :, 0:nq]
            es3 = mk_ap(esink, esink[:, l * 4:l * 4 + 4].offset, [[8, 128], [1, 4], [0, nq]])
            tt("dve", rc3, pD3, es3, ALU.add, (BpDa, B_esink), (Brc,))
            recip(rc3, rc3, (Brc,), (Brc,))
            tt("dve", yaT[:, :, q0:q0 + nq], pO3, rc3, ALU.mult, (BpOa, Brc), (Bya,))

        for kv in range(2):
            qk, pv = make_a(kv)
            units.append((qk, pv, post_a if kv == 1 else None))
        return units

    def run_units(units, depth=1):
        n = len(units)
        for i in range(n + depth):
            if i < n:
                units[i][0]()
            j = i - depth
            if j >= 0:
                units[j][1]()
                if units[j][2] is not None:
                    units[j][2]()

    ch_kout = P.chan("kout")
    ch_vout = P.chan("vout")
    ch_kaout = P.chan("kaout")
    ch_vaout = P.chan("vaout")

    def store_k_band(l, tinfo, kout, Bk, nb):
        if tinfo["kind"] == "prompt":
            dma("sp", kbp_o[l].rearrange("(b p) c -> p b c", p=128), kout[:, 0:4, :], ch_kout, (Bk,), ())
        else:
            for s in range(NS):
                base, tbk = (s % 2) * 64, s // 2
                dma("sp", kbs_o[l, s, 448:512, :], kout[base:base + 64, tbk, :], ch_kout, (Bk,), ())

    def store_v_band(l, tinfo, vout, Bv, nb):
        if tinfo["kind"] == "prompt":
            dma("sp", vbp_o[l].rearrange("(b p) c -> p b c", p=128), vout[:, 0:4, :], ch_vout, (Bv,), ())
        else:
            for s in range(NS):
                base, tbk = (s % 2) * 64, s // 2
                dma("sp", vbs_o[l, s, 448:512, :], vout[base:base + 64, tbk, :], ch_vout, (Bv,), ())

    def store_win(l, tinfo, nb):
        if tinfo["kind"] == "prompt":
            dma("sp", kwp_o[l], kaout[:, 3, :], ch_kaout, (B_kaout,), ())
            dma("sp", vwp_o[l], vaout[:, 3, :], ch_vaout, (B_vaout,), ())
        else:
            for s in range(NS):
                base, tbk = (s % 2) * 64, s // 2
                dma("sp", kws_o[l, s, 64:128, :], kaout[base:base + 64, tbk, :], ch_kaout, (B_kaout,), ())
                dma("sp", vws_o[l, s, 64:128, :], vaout[base:base + 64, tbk, :], ch_vaout, (B_vaout,), ())

    knewb = sb("knewb", [128, 4, SN], BF16)
    knewa = sb("knewa", [128, SN], BF16)
    vnewb = sb("vnewb", [128, 2, 512], BF16)
    vnewa = sb("vnewa", [128, 2, 128], BF16)
    zmask = sb("zmask", [128, 128], F32)
    t34_hi = [sb("t34hi_%d" % l, [128, 1, 8, 64], BF16) for l in range(2)]
    B_knb, B_kna, B_vnb, B_vna = Buf("knb"), Buf("kna"), Buf("vnb"), Buf("vna")
    memset("pool", zmask[:], 0.0, (B_masks,))
    toep_load(256, 0, 64, 64, lambda l: t34_hi[l][64:128, 0, :, :])

    ch_cvb = [P.chan("cvb%d" % i) for i in range(2)]
    ch_cva = [P.chan("cva%d" % i) for i in range(2)]
    ch_ckb = P.chan("ckbst")
    ch_cka = P.chan("ckast")

    def load_cache(l, s, rs, Bstg):
        dma("pool", vbr[rs][:, 0:4, :], cvb[l, s].rearrange("(b p) c -> p b c", p=128), ch_cvb[rs], (), (B_vbr[rs],))
        dma("pool", var_[rs][:, 0, :], cvw[l, s], ch_cva[rs], (), (B_var[rs],))
        for half in range(2):
            kc2 = arenaG[:, 16 * TN:16 * TN + 2 * 2 * 512].bitcast(F32).rearrange("p (b c) -> p b c", b=2)
            dma("sp", kc2, ckb[l, s, half * 256:(half + 1) * 256, :].rearrange("(b p) c -> p b c", p=128),
                ch_ckb, (), (Bstg,))
            for hp in range(4):
                pt, Bp = psum()
                for tb in range(2):
                    tr(pt[:, tb * 128:(tb + 1) * 128], kc2[:, tb, hp * 128:(hp + 1) * 128], ident[:, :],
                       (Bstg, B_ident), (Bp,))
                cp("act", kbT[rs][:, hp, half * 256:(half + 1) * 256], pt[:, 0:256], (Bp,), (B_kbT[rs],))
        dma("sp", kcs_a[:], ckw[l, s], ch_cka, (), (B_kcsa,))
        pt, Bp = psum()
        tr(pt[:, 0:128], kcs_a[:, :], ident[:, :], (B_kcsa, B_ident), (Bp,))
        cp("act", kaT[rs][:, 0:128], pt[:, 0:128], (Bp,), (B_kaT[rs],))

    ch_x = P.chan("xin")
    ch_x2 = P.chan("xin2")
    ch_pos = P.chan("pos")
    ch_vld = P.chan("vldin")
    ch_y = P.chan("yout")

    xst_views = [arenaU[:, :].bitcast(F32).rearrange("p (b c) -> p b c", b=2),
                 xn[:, :, :].rearrange("p c t -> p (c t)").bitcast(F32).rearrange("p (b c) -> p b c", b=2)]
    xpre = {"bufs": None}

    def issue_x_dma(N, src_rows_ap):
        nb = N // 128
        P.tag = "load_x"
        BstA = retarget(B_U, "xstageA")
        dma("sp", xst_views[0], src_rows_ap[0:256, :].rearrange("(b p) c -> p b c", p=128), ch_x, (), (BstA,))
        if nb > 2:
            dma("sp", xst_views[1], src_rows_ap[256:512, :].rearrange("(b p) c -> p b c", p=128), ch_x2, (),
                tuple(B_xn))
        xpre["bufs"] = BstA

    def load_x(N, src_rows_ap):
        nb = N // 128
        if xpre["bufs"] is None:
            issue_x_dma(N, src_rows_ap)
        BstA = xpre["bufs"]
        xpre["bufs"] = None
        P.tag = "load_x"
        for c in range(8):
            pt, Bp = psum()
            for tb in range(nb):
                if tb < 2:
                    tr(pt[:, tb * 128:(tb + 1) * 128], xst_views[0][:, tb, c * 128:(c + 1) * 128], ident[:, :],
                       (BstA, B_ident), (Bp,))
                else:
                    tr(pt[:, tb * 128:(tb + 1) * 128], xst_views[1][:, tb - 2, c * 128:(c + 1) * 128], ident[:, :],
                       tuple(B_xn) + (B_ident,), (Bp,))
            if c % 2 == 0:
                cp("act", xT[:, c, 0:N], pt[:, 0:N], (Bp,), (B_xT[c],))
            else:
                cp("dve", xT[:, c, 0:N], pt[:, 0:N], (Bp,), (B_xT[c],))

    def final_out(N, dst_rows_ap):
        nb = N // 128
        P.tag = "final"
        Bst = retarget(B_G, "ystage")
        yst = arenaG[:, 0:nb * 2 * D].bitcast(F32).rearrange("p (b c) -> p b c", b=nb)
        get_rstd(N)
        gcol = C_NORM + 6 * 8
        for c in range(8):
            yn, Byn = tmpf()
            stt(yn[:, 0:N], xT[:, c, 0:N], smp[:, gcol + c:gcol + c + 1], rstd_sb[:, 0:N], ALU.mult, ALU.mult,
                (B_xT[c], B_rstd, B_smp), (Byn,))
            ptr, Bptr = psum()
            for tb in range(nb):
                tr(ptr[:, tb * 128:(tb + 1) * 128], yn[:, tb * 128:(tb + 1) * 128], ident[:, :], (Byn, B_ident), (Bptr,))
            eng = "act" if c % 2 == 0 else "dve"
            cp(eng, yst[:, :, c * 128:(c + 1) * 128], ptr[:, 0:N].rearrange("p (b c) -> p b c", b=nb), (Bptr,), (Bst,))
        dma("sp", dst_rows_ap.rearrange("(b p) c -> p b c", p=128), yst, ch_y, (Bst,), ())

    def run_tile(N, tinfo, x_rows_ap, pos_off, y_rows_ap, next_x=None):
        stats["ready"] = None
        load_x(N, x_rows_ap)
        if stop_after == "load":
            return
        rope_tables(N, pos_off)
        if stop_after == "rope":
            return
        if tinfo["kind"] == "prompt":
            t = tinfo["t"]
            rb = (t % 2) * 4
            dma("pool", vldr[:, rb:rb + 4, :], vldd[t * TN:(t + 1) * TN, :].rearrange("(b p) c -> p b c", p=128),
                ch_vld, (), (B_vld,))
        modes = tile_modes(tinfo["kind"], tinfo["t"])
        for l in range(n_layers):
            mode = modes[l]
            if mode == "NONE":
                continue
            ws["pos"] = 0
            ws["lbase"] = l * NSLOT_L
            tinfo["mode"] = mode
            ffn(N, C_NORM + (l * 3 + 0) * 8)
            assert ws["pos"] == 66
            if stop_after == "ffn1":
                return
            mixer(N, l, tinfo)
            if stop_after in ("mixer", "mixproj", "attn"):
                return
            assert ws["pos"] == 121, ws["pos"]
            if mode == "FULL":
                ffn(N, C_NORM + (l * 3 + 2) * 8)
                assert ws["pos"] == 187
        if next_x is not None:
            issue_x_dma(next_x[0], next_x[1])
        if y_rows_ap is not None:
            final_out(N, y_rows_ap)

    for t in range(n_ptiles):
        is_last = (t == n_ptiles - 1)
        yr = y_o[(t - 2) * TN:(t - 1) * TN, :] if t >= 2 else None
        if t + 1 < n_ptiles:
            nx = (TN, xseg[(t + 1) * TN:(t + 2) * TN, :])
        elif with_sample:
            nx = (SN, xsmp[:, :])
        else:
            nx = None
        if stop_after is not None:
            nx = None
        run_tile(TN, dict(kind="prompt", t=t, out=is_last), xseg[t * TN:(t + 1) * TN, :], t * TN, yr, next_x=nx)
    if with_sample:
        run_tile(SN, dict(kind="sample", t=None, out=True), xsmp[:, :], SEG, ys_o[:, :])

    counts, nwait = P.emit()
    _CACHE["pe_tags"] = [e.tag for e in P.entries if e.eng == "pe"]
    _CACHE["tags"] = {k: [e.tag for e in P.entries if e.eng == k] for k in ("act", "dve")}
    return nc, counts, nwait, len(P.entries)


def _f1(W, cols):
    K = W.shape[0]
    blk = W[:, cols].reshape(K // 128, 128, len(cols)).transpose(1, 0, 2)
    return np.ascontiguousarray(blk).reshape(128, -1)


def _weight_stream(inp):
    blocks = np.zeros((2, NBLK, 128, 1024), np.float32)
    sw = np.concatenate([np.arange(8, 16), np.arange(0, 8), np.arange(16, 64)])
    for l in range(2):
        b = 0
        def put(arr):
            nonlocal b
            blocks[l, b] = arr
            b += 1
        def ffn_blocks(win, wout):
            for i in range(NF):
                put(_f1(win, np.arange(i * 128, (i + 1) * 128)))
                put(_f1(win, np.arange(DFF + i * 128, DFF + (i + 1) * 128)))
            for i in range(NF):
                put(wout[i * 128:(i + 1) * 128, :])
        ffn_blocks(inp["w_ffn1_in"][l], inp["w_ffn1_out"][l])
        W = inp["w_mix_in"][l]
        for j in range(4):
            c = np.concatenate([np.arange(j * 64, (j + 1) * 64), np.arange((4 + j) * 64, (5 + j) * 64)])
            put(_f1(W, c))
            c2 = np.concatenate([j * 64 + sw, (4 + j) * 64 + sw])
            put(_f1(W, c2))
        put(_f1(W, np.arange(512, 640)))
        put(_f1(W, np.concatenate([512 + sw, 576 + sw])))
        assert b % 4 == 0
        vb = W[:, 1792:2304].reshape(8, 128, 512).transpose(1, 0, 2).reshape(128, 4096)
        for k in range(4):
            put(vb[:, k * 1024:(k + 1) * 1024])
        put(_f1(W, np.arange(640, 768)))
        for j in range(4):
            put(_f1(W, np.arange(768 + j * 128, 768 + (j + 1) * 128)))
        for j in range(4):
            put(_f1(W, np.arange(1280 + j * 128, 1280 + (j + 1) * 128)))
        WA = inp["w_branch_a"][l]
        WB = inp["w_branch_b"][l]
        rows_a = np.concatenate([np.concatenate([np.arange(kc * 64, (kc + 1) * 64),
                                                 np.arange((4 + kc) * 64, (5 + kc) * 64)]) for kc in range(4)])
        for j in range(8):
            put(_f1(W, np.arange(2304 + j * 128, 2304 + (j + 1) * 128)))
            put(_f1(W, np.arange(3328 + j * 128, 3328 + (j + 1) * 128)))
            br = np.concatenate([_f1(WA[rows_a], np.arange(j * 128, (j + 1) * 128)),
                                 _f1(WB, np.arange(j * 128, (j + 1) * 128))], axis=1)
            put(br)
        WO = inp["w_mix_out"][l]
        for j in range(8):
            put(_f1(WO, np.arange(j * 128, (j + 1) * 128)))
        ffn_blocks(inp["w_ffn2_in"][l], inp["w_ffn2_out"][l])
        assert b == 187, b
    s = blocks.reshape(2 * NSLOT_L, SLOT, 128, 1024).transpose(0, 2, 1, 3).reshape(2 * NSLOT_L, 128, SLOT * 1024)
    return np.ascontiguousarray(s)


def _small_params(inp):
    sp = np.zeros((128, NCOLS), np.float32)
    norms = [inp["norm_ffn1"][0], inp["norm_mix"][0], inp["norm_ffn2"][0],
             inp["norm_ffn1"][1], inp["norm_mix"][1], inp["norm_ffn2"][1], inp["norm_final"]]
    for i, g in enumerate(norms):
        sp[:, C_NORM + i * 8:C_NORM + (i + 1) * 8] = g.reshape(8, 128).T
    for l in range(2):
        sp[:, C_BG + l * 16:C_BG + (l + 1) * 16] = inp["b_gate"][l].reshape(16, 128).T
        for j in range(4):
            sp[0:64, C_SINK + l * 4 + j] = inp["sinks"][l, j]
            sp[64:128, C_SINK + l * 4 + j] = inp["sinks"][l, 4 + j]
        for h in range(8):
            sp[:, C_BC + l * 8 + h] = inp["rel_bias"][l, h, 0]
    d = np.arange(128) % 64
    sp[:, C_RI] = np.where(d < 16, d % 8, 0)
    sp[:, C_RM] = (d < 16)
    sp[:, C_RS] = np.where(d < 8, -1.0, np.where(d < 16, 1.0, 0.0))
    sp[:, C_RO] = 1.0 - (d < 16)
    return sp


_CACHE = {}


def kernel(**inputs):
    inp = {k: np.asarray(v) for k, v in inputs.items()}
    if "nc" not in _CACHE:
        _CACHE["nc"] = build()[0]
    nc = _CACHE["nc"]
    wst = _weight_stream(inp)
    smallp = _small_params(inp)
    m = np.arange(385)
    ext_idx = np.maximum(m - 128, 0)
    rev = inp["rel_bias"][:, :, ext_idx][:, :, ::-1]
    relx = np.ascontiguousarray(rev.reshape(16, 385).T.reshape(1, 16 * 385)).astype(np.float32)
    ident = np.eye(128, dtype=np.float32)
    in_maps = []
    for c in range(8):
        b, half = c // 2, c % 2
        start = half * OWN - HALO
        xseg = np.zeros((SEG, D), np.float32)
        lo = max(start, 0)
        xseg[lo - start:] = inp["x_prompt"][b, lo:start + SEG]
        posv = np.arange(start, start + SEG, dtype=np.float32)
        vld = np.repeat((posv >= 0).astype(np.float32)[:, None], 64, axis=1)
        pos = np.concatenate([posv, np.tile(np.arange(PAST, PAST + 64, dtype=np.float32), NS)])[None, :]
        sl = slice(c * NS, (c + 1) * NS)
        in_maps.append({
            "xseg": xseg,
            "xsmp": np.ascontiguousarray(inp["x_sample"][sl].reshape(SN, D)),
            "pos": np.ascontiguousarray(pos),
            "vld": np.ascontiguousarray(vld),
            "wst": wst,
            "smallp": smallp,
            "relx": relx,
            "ident": ident,
            "ckw": np.ascontiguousarray(inp["cache_k_win"][:, sl].reshape(2, NS, 128, 128)),
            "cvw": np.ascontiguousarray(inp["cache_v_win"][:, sl].reshape(2, NS, 128, 128)),
            "ckb": np.ascontiguousarray(inp["cache_k_band"][:, sl].reshape(2, NS, 512, 512)),
            "cvb": np.ascontiguousarray(inp["cache_v_band"][:, sl].reshape(2, NS, 512, 512)),
        })
    if "dbg_cores" in _CACHE:
        sel = _CACHE["dbg_cores"]
        res = run_bass_kernel_spmd(nc, [in_maps[i] for i in sel], core_ids=list(range(len(sel))))
        R = [res.results[sel.index(i)] if i in sel else res.results[0] for i in range(8)]
    else:
        res = run_bass_kernel_spmd(nc, in_maps, core_ids=list(range(8)))
        R = res.results
    y_prompt = np.zeros((4, SEQ, D), np.float32)
    y_sample = np.zeros((32, 64, D), np.float32)
    kwp = np.zeros((2, 4, 128, 2, 64), np.float32)
    vwp = np.zeros((2, 4, 128, 2, 64), np.float32)
    kbp = np.zeros((2, 4, 512, 8, 64), np.float32)
    vbp = np.zeros((2, 4, 512, 8, 64), np.float32)
    kws = np.zeros((2, 32, 128, 2, 64), np.float32)
    vws = np.zeros((2, 32, 128, 2, 64), np.float32)
    kbs = np.zeros((2, 32, 512, 8, 64), np.float32)
    vbs = np.zeros((2, 32, 512, 8, 64), np.float32)
    for c in range(8):
        b, half = c // 2, c % 2
        r = R[c]
        y_prompt[b, half * OWN:(half + 1) * OWN] = r["y"]
        sl = slice(c * NS, (c + 1) * NS)
        y_sample[sl] = r["ys"].reshape(NS, 64, D)
        if half == 1:
            kwp[:, b] = r["kwp"].reshape(2, 128, 2, 64)
            vwp[:, b] = r["vwp"].reshape(2, 128, 2, 64)
            kbp[:, b] = r["kbp"].reshape(2, 512, 8, 64)
            vbp[:, b] = r["vbp"].reshape(2, 512, 8, 64)
        kws[:, sl] = r["kws"].reshape(2, NS, 128, 2, 64)
        vws[:, sl] = r["vws"].reshape(2, NS, 128, 2, 64)
        kbs[:, sl] = r["kbs"].reshape(2, NS, 512, 8, 64)
        vbs[:, sl] = r["vbs"].reshape(2, NS, 512, 8, 64)
    return (y_prompt, y_sample, kwp, vwp, kbp, vbp, kws, vws, kbs, vbs)
```

```python
import math
import numpy as np
import concourse.bass as bass
import concourse.mybir as mybir
from concourse.bass_utils import run_bass_kernel_spmd

F32 = mybir.dt.float32
BF16 = mybir.dt.bfloat16
I32 = mybir.dt.int32
ALU = mybir.AluOpType
AF = mybir.ActivationFunctionType

D = 1024
DFF = 2816
NF = 22
SEQ = 8192
OWN = 4096
HALO = 1024
SEG = OWN + HALO
TN = 512
NPT = SEG // TN
NS = 4
SN = NS * 64
PAST = 4096
NBLK = 188
SLOT = 4
NSLOT_L = NBLK // SLOT
NRING = 5
EPS = 1e-5
NEG = -30000.0
THETA = 500000.0
TWO_PI = 2.0 * math.pi

C_NORM = 0
C_BG = 56
C_SINK = 88
C_BC = 96
C_RI = 112
C_RM = 113
C_RS = 114
C_RO = 115
NCOLS = 116


class Buf:
    __slots__ = ("name", "w", "r", "track", "excl")

    def __init__(self, name, track=True, excl=False):
        self.name = name
        self.w = None
        self.r = []
        self.track = track
        self.excl = excl


class Entry:
    __slots__ = ("eng", "fn", "deps", "marked", "ev", "is_dma", "chan", "idx", "key", "tag")


class Chan:
    def __init__(self, nc, name):
        self.sem = nc.alloc_semaphore(name)
        self.cnt = 0
        self.key = name


class Prog:
    def __init__(self, nc):
        self.nc = nc
        self.entries = []
        self.engs = {"pe": nc.tensor, "act": nc.scalar, "dve": nc.vector, "pool": nc.gpsimd, "sp": nc.sync}
        self.sems = {k: nc.alloc_semaphore("sem_" + k) for k in self.engs}
        self.nchan = 0
        self.chans = []
        self.tag = ""

    def chan(self, name):
        c = Chan(self.nc, "ch%d_%s" % (self.nchan, name))
        self.nchan += 1
        self.chans.append(c)
        return c

    def op(self, eng, fn, reads=(), writes=(), chan=None, nowaw=False):
        e = Entry()
        e.eng = eng
        e.fn = fn
        e.is_dma = chan is not None
        e.marked = False
        e.chan = chan
        e.ev = None
        e.idx = len(self.entries)
        e.tag = self.tag
        e.key = chan.key if chan is not None else eng
        deps = {}

        def add(d):
            o = deps.get(d.key)
            if o is None or o.idx < d.idx:
                deps[d.key] = d
        for b in reads:
            if b.w is not None:
                add(b.w)
            if b.excl:
                for x in b.r:
                    if x.key != e.key:
                        add(x)
        for b in writes:
            if b.w is not None and not nowaw:
                add(b.w)
            for x in b.r:
                add(x)
        dl = []
        for d in deps.values():
            if d.eng == "pe" and eng == "pe" and not d.is_dma and chan is None:
                continue
            dl.append(d)
        e.deps = dl
        if chan is not None:
            chan.cnt += 1
            e.ev = (chan.sem, 16 * chan.cnt, chan.key)
        for b in reads:
            if b.track:
                b.r = [x for x in b.r if x.key != e.key]
                b.r.append(e)
        for b in writes:
            b.w = e
            b.r = []
        self.entries.append(e)
        return e

    def emit(self):
        for e in self.entries:
            for d in e.deps:
                if not d.is_dma:
                    d.marked = True
        counts = {k: 0 for k in self.engs}
        for e in self.entries:
            if not e.is_dma and e.marked:
                counts[e.eng] += 1
                e.ev = (self.sems[e.eng], counts[e.eng], e.eng)
        for k, v in counts.items():
            assert v < 60000, (k, v)
        waited = {k: {} for k in self.engs}
        nwait = 0
        for e in self.entries:
            h = self.engs[e.eng]
            need = {}
            for d in e.deps:
                sem, val, key = d.ev
                if need.get(key, (None, 0))[1] < val:
                    need[key] = (sem, val)
            w = waited[e.eng]
            for key, (sem, val) in need.items():
                if w.get(key, 0) < val:
                    h.wait_ge(sem, val)
                    w[key] = val
                    nwait += 1
            inst = e.fn()
            if e.is_dma:
                inst.then_inc(e.ev[0], 16)
            elif e.marked:
                inst.then_inc(e.ev[0], 1)
        sp = self.engs["sp"]
        for c in self.chans:
            if c.cnt > 0:
                sp.wait_ge(c.sem, 16 * c.cnt)
        return counts, nwait


def mk_ap(t, offset, ap):
    return bass.AP(tensor=t, offset=offset, ap=[list(x) for x in ap])


def build(n_ptiles=NPT, with_sample=True, n_layers=2, stop_after=None, halo_skip=True):
    nc = bass.Bass("TRN2", target_bir_lowering=False)
    P = Prog(nc)

    def dram_in(name, shape, dt=F32):
        return nc.dram_tensor(name, list(shape), dt, kind="ExternalInput")

    def dram_out(name, shape):
        return nc.dram_tensor(name, list(shape), F32, kind="ExternalOutput")

    xseg = dram_in("xseg", [SEG, D])
    xsmp = dram_in("xsmp", [SN, D])
    posd = dram_in("pos", [1, SEG + SN])
    vldd = dram_in("vld", [SEG, 64])
    wst = dram_in("wst", [2 * NSLOT_L, 128, SLOT * 1024])
    smallp = dram_in("smallp", [128, NCOLS])
    relx = dram_in("relx", [1, 16 * 385])
    identd = dram_in("ident", [128, 128])
    ckw = dram_in("ckw", [2, NS, 128, 128])
    cvw = dram_in("cvw", [2, NS, 128, 128])
    ckb = dram_in("ckb", [2, NS, 512, 512])
    cvb = dram_in("cvb", [2, NS, 512, 512])

    y_o = dram_out("y", [OWN, D])
    ys_o = dram_out("ys", [SN, D])
    kwp_o = dram_out("kwp", [2, 128, 128])
    vwp_o = dram_out("vwp", [2, 128, 128])
    kbp_o = dram_out("kbp", [2, 512, 512])
    vbp_o = dram_out("vbp", [2, 512, 512])
    kws_o = dram_out("kws", [2, NS, 128, 128])
    vws_o = dram_out("vws", [2, NS, 128, 128])
    kbs_o = dram_out("kbs", [2, NS, 512, 512])
    vbs_o = dram_out("vbs", [2, NS, 512, 512])

    wbf = nc.dram_tensor("wbf", [2 * NSLOT_L, 128, SLOT * 1024], BF16)
    rrep = nc.dram_tensor("rrep", [128, 16 * 385], F32)

    def sb(name, shape, dt):
        return nc.alloc_sbuf_tensor(name, list(shape), dt)

    xT = sb("xT", [128, 8, TN], F32)
    xn = sb("xn", [128, 8, TN], BF16)
    arenaG = sb("arenaG", [128, 22 * TN], BF16)
    arenaU = sb("arenaU", [128, 8 * TN], BF16)
    wring = sb("wring", [128, NRING, SLOT * 1024], BF16)
    kbT = [sb("kbT%d" % l, [128, 4, 1024], BF16) for l in range(2)]
    kaT = [sb("kaT%d" % l, [128, 1024], BF16) for l in range(2)]
    vbr = [sb("vbr%d" % l, [128, 8, 512], BF16) for l in range(2)]
    var_ = [sb("var%d" % l, [128, 8, 128], BF16) for l in range(2)]
    vldr = sb("vldr", [128, 8, 64], BF16)
    ones64 = sb("ones64", [128, 64], BF16)
    onesb = sb("onesb", [128, 128], BF16)
    ident = sb("ident_sb", [128, 128], F32)
    identb = sb("identb", [128, 128], BF16)
    smp = sb("smp", [128, NCOLS], F32)
    esink = sb("esink", [128, 8], F32)
    freq = sb("freq", [128, 1], F32)
    tb3 = [sb("tb3_%d" % l, [128, 8, 3, 128], BF16) for l in range(2)]
    mask_lo = sb("mask_lo", [128, 128], F32)
    mask_hi = sb("mask_hi", [128, 128], F32)
    cosT = sb("cosT", [128, TN], F32)
    sinT = sb("sinT", [128, TN], F32)
    posb = sb("posb", [128, TN], F32)
    NTMP = 6
    tmp32 = [sb("tmp32_%d" % i, [128, TN], F32) for i in range(NTMP)]
    tmpi = sb("tmpi", [128, TN], I32)
    rstd_sb = sb("rstd_sb", [128, TN], F32)
    B_rstd = Buf("rstd")
    pTb = [sb("pTb%d" % i, [128, 5, 128], BF16) for i in range(4)]
    pTa = [sb("pTa%d" % i, [128, 2, 4, 128], BF16) for i in range(3)]
    kaout = sb("kaout", [128, 4, 128], F32)
    vaout = sb("vaout", [128, 4, 128], F32)
    kcs_a = sb("kcs_a", [128, 128], F32)

    ps = nc.alloc_psum_tensor("ps", [128, 8 * 512], F32)

    B_xT = [Buf("xT%d" % c) for c in range(8)]
    B_xn = [Buf("xn%d" % c) for c in range(8)]
    stats = {"ready": None}
    B_G = {"cur": Buf("arenaG")}
    B_U = {"cur": Buf("arenaU")}
    B_ring = [Buf("ring%d" % i) for i in range(NRING)]
    B_kbT = [Buf("kbT%d" % l) for l in range(2)]
    B_kaT = [Buf("kaT%d" % l) for l in range(2)]
    B_vbr = [Buf("vbr%d" % l) for l in range(2)]
    B_var = [Buf("var%d" % l) for l in range(2)]
    B_vld = Buf("vldr")
    B_smp = Buf("smp", track=False)
    B_ident = Buf("ident", track=False)
    B_ones = Buf("ones", track=False)
    B_masks = Buf("masks", track=False)
    B_esink = Buf("esink", track=False)
    B_freq = Buf("freq", track=False)
    B_t34 = Buf("t34", track=False)
    B_rope = Buf("rope")
    B_posb = Buf("posb")
    B_tmp = [Buf("tmp%d" % i) for i in range(NTMP)]
    B_tmpi = Buf("tmpi")
    B_pTb = [Buf("pTb%d" % i) for i in range(4)]
    B_pTa = [Buf("pTa%d" % i) for i in range(3)]
    B_kaout = Buf("kaout")
    B_vaout = Buf("vaout")
    B_kcsa = Buf("kcsa")
    B_ps = [Buf("ps%d" % i, excl=True) for i in range(8)]
    B_wbf = [Buf("wbf%d" % i) for i in range(16)]
    B_rrep = Buf("rrep")
    B_dummy_out = Buf("dram_out")

    st = {"ps": 0, "tmp": 0, "ptb": 0, "pta": 0}

    def psum():
        i = st["ps"] % 4
        st["ps"] += 1
        return ps[:, i * 512:(i + 1) * 512], B_ps[i]

    def psum_fixed(i):
        return ps[:, i * 512:(i + 1) * 512], B_ps[i]

    def tmpf():
        i = st["tmp"] % NTMP
        st["tmp"] += 1
        return tmp32[i], B_tmp[i]

    def mm(out, lhsT, rhs, start, stop, reads, writes):
        P.op("pe", lambda: nc.tensor.matmul(out, lhsT=lhsT, rhs=rhs, start=start, stop=stop), reads, writes)

    def tr(out, in_, idn, reads, writes):
        P.op("pe", lambda: nc.tensor.transpose(out, in_, idn), reads, writes)

    def act(out, in_, func, reads, writes, bias=None, scale=None):
        kw = {}
        if bias is not None:
            kw["bias"] = bias
        if scale is not None:
            kw["scale"] = scale
        P.op("act", lambda: nc.scalar.activation(out=out, in_=in_, func=func, **kw), reads, writes)

    def tt(eng, out, in0, in1, op, reads, writes):
        h = nc.vector if eng == "dve" else nc.gpsimd
        P.op(eng, lambda: h.tensor_tensor(out=out, in0=in0, in1=in1, op=op), reads, writes)

    def ts(eng, out, in0, s1, s2, op0, op1, reads, writes):
        h = nc.vector if eng == "dve" else nc.gpsimd
        if op1 is None:
            P.op(eng, lambda: h.tensor_scalar(out=out, in0=in0, scalar1=s1, scalar2=None, op0=op0), reads, writes)
        else:
            P.op(eng, lambda: h.tensor_scalar(out=out, in0=in0, scalar1=s1, scalar2=s2, op0=op0, op1=op1), reads, writes)

    def stt(out, in0, scalar, in1, op0, op1, reads, writes):
        P.op("dve", lambda: nc.vector.scalar_tensor_tensor(out=out, in0=in0, scalar=scalar, in1=in1, op0=op0, op1=op1),
             reads, writes)

    def cp(eng, out, in_, reads, writes):
        if eng == "act":
            P.op("act", lambda: nc.scalar.copy(out=out, in_=in_), reads, writes)
        else:
            h = nc.vector if eng == "dve" else nc.gpsimd
            P.op(eng, lambda: h.tensor_copy(out=out, in_=in_), reads, writes)

    def recip(out, in_, reads, writes):
        P.op("dve", lambda: nc.vector.reciprocal(out=out, in_=in_), reads, writes)

    def memset(eng, ap, val, writes):
        h = nc.vector if eng == "dve" else nc.gpsimd
        P.op(eng, lambda: h.memset(ap, val), (), writes)

    def dma(q, out, in_, chan, reads, writes, nowaw=False):
        h = nc.sync if q == "sp" else nc.gpsimd
        P.op(q, lambda: h.dma_start(out=out, in_=in_), reads, writes, chan=chan, nowaw=nowaw)

    def retarget(box, name):
        old = box["cur"]
        nb = Buf(name)
        nb.r = list(old.r) + ([old.w] if old.w is not None else [])
        box["cur"] = nb
        return nb

    def gview_f32(off_bytes, shape):
        v = arenaG[:, off_bytes // 2: off_bytes // 2 + 2 * int(np.prod(shape[1:]))].bitcast(F32)
        return v

    ch_c = P.chan("t34")
    ch_smp = P.chan("smp")
    ch_id = P.chan("ident")
    dma("sp", smp[:], smallp.ap(), ch_smp, (), (B_smp,))
    dma("sp", ident[:], identd.ap(), ch_id, (), (B_ident,))
    cp("dve", identb[:], ident[:], (B_ident,), (B_ones,))
    memset("pool", onesb[:], 1.0, (B_ones,))
    memset("pool", ones64[:], 1.0, (B_ones,))
    memset("pool", mask_lo[:], 0.0, (B_masks,))
    memset("pool", mask_hi[:], 0.0, (B_masks,))
    memset("pool", mask_lo[0:64, 64:128], NEG, (B_masks,))
    memset("pool", mask_hi[64:128, 0:64], NEG, (B_masks,))
    for l in range(2):
        memset("pool", kbT[l][:], 0.0, (B_kbT[l],))
        memset("pool", kaT[l][:], 0.0, (B_kaT[l],))
        memset("pool", vbr[l][:], 0.0, (B_vbr[l],))
        memset("pool", var_[l][:], 0.0, (B_var[l],))
    memset("pool", vldr[:], 0.0, (B_vld,))
    act(esink[:], smp[:, C_SINK:C_SINK + 8], AF.Exp, (B_smp,), (B_esink,))
    act(freq[:], smp[:, C_RI:C_RI + 1], AF.Exp, (B_smp,), (B_freq,), scale=-math.log(THETA) / 8.0)
    ch_r = P.chan("rrep")
    dma("sp", rrep.ap(), mk_ap(relx, 0, [[0, 128], [1, 16 * 385]]), ch_r, (), (B_rrep,))
    L_ = 385

    tzs = sb("tzs", [128, 1024], F32)
    B_tzs = Buf("tzs")
    ch_tz = P.chan("tz")

    def toep_load(cc, j0, npart, p0, conv):
        src = mk_ap(rrep, (L_ - 1 - cc + j0) * 16, [[16 * L_ - 16, npart], [1, 1024]])
        dma("sp", tzs[p0:p0 + npart, :], src, ch_tz, (B_rrep,), (B_tzs,))
        for l in range(2):
            sv = mk_ap(tzs, p0 * 1024 + l * 8, [[1024, npart], [1, 8], [16, 64]])
            ts("dve", conv(l), sv, 8.0, None, ALU.mult, None, (B_tzs, B_t34), (B_t34,))

    def build_toeplitz():
        for bi, cc in ((1, 128), (2, 256)):
            for jh in range(2):
                toep_load(cc, jh * 64, 128, 0, lambda l, bi=bi, jh=jh: tb3[l][:, :, bi, jh * 64:(jh + 1) * 64])
        for l in range(2):
            ts("dve", tb3[l][64:128, :, 2, 0:64], tb3[l][64:128, :, 2, 0:64], 8.0 * NEG, None, ALU.add, None, (B_t34,), (B_t34,))
            for h in range(8):
                ts("dve", tb3[l][:, h, 0, :], mask_lo[:, :], smp[:, C_BC + l * 8 + h:C_BC + l * 8 + h + 1], 8.0, ALU.add,
                   ALU.mult, (B_masks, B_smp, B_t34), (B_t34,))


    gsz = [1, 1, 2, 4] + [8] * 11
    grp_of = []
    for gi_, n_ in enumerate(gsz):
        grp_of.extend([gi_] * n_)
    grp_of = grp_of[:2 * NSLOT_L]
    ch_pc = [P.chan("pc%d" % i) for i in range(len(gsz))]
    for s in range(2 * NSLOT_L):
        gi = grp_of[s]
        dma("pool", wbf[s], wst[s], ch_pc[gi], (), (B_wbf[gi],), nowaw=True)

    ch_cc = P.chan("cachecopy")
    if with_sample:
        for l in range(2):
            dma("pool", kws_o[l, :, 0:64, :], ckw[l, :, 64:128, :], ch_cc, (), ())
            dma("pool", vws_o[l, :, 0:64, :], cvw[l, :, 64:128, :], ch_cc, (), ())
            for s in range(NS):
                dma("pool", kbs_o[l, s, 0:448, :], ckb[l, s, 64:512, :], ch_cc, (), ())
                dma("pool", vbs_o[l, s, 0:448, :], cvb[l, s, 64:512, :], ch_cc, (), ())

    ch_ring = [P.chan("ring%d" % i) for i in range(NRING)]
    ws = {"issued": 0, "pos": 0, "cur": 0, "lbase": 0}

    def tile_modes(kind, t):
        if kind == "sample" or not halo_skip:
            return ["FULL"] * n_layers
        if t == 0:
            return (["KV", "NONE"])[:n_layers]
        if t == 1:
            return (["FULL", "KV"])[:n_layers]
        return ["FULL"] * n_layers

    def layer_slots(mode):
        if mode == "FULL":
            return list(range(NSLOT_L))
        if mode == "KV":
            return list(range(17)) + [18, 19, 20, 21, 22]
        return []

    sched = []
    for t_ in range(n_ptiles):
        for l_, m_ in enumerate(tile_modes("prompt", t_)):
            sched.extend(l_ * NSLOT_L + x for x in layer_slots(m_))
    if with_sample:
        for l_, m_ in enumerate(tile_modes("sample", None)):
            sched.extend(l_ * NSLOT_L + x for x in layer_slots(m_))

    def issue_slot():
        g = ws["issued"]
        if g >= len(sched):
            return
        s_ = sched[g]
        r = g % NRING
        dma("sp", wring[:, r, :], wbf[s_], ch_ring[r], (B_wbf[grp_of[s_]],), (B_ring[r],))
        ws["issued"] += 1

    early_x = {"fn": None}

    def next_blocks(n):
        outl = []
        for _ in range(n):
            b = ws["pos"]
            dslot = ws["lbase"] + b // SLOT
            if sched[ws["cur"]] != dslot:
                ws["cur"] += 1
            assert sched[ws["cur"]] == dslot, (ws["cur"], sched[ws["cur"]], dslot)
            while ws["issued"] <= ws["cur"] + NRING - 2 and ws["issued"] < len(sched):
                issue_slot()
            r = ws["cur"] % NRING
            k = b % SLOT
            outl.append((wring[:, r, k * 1024:(k + 1) * 1024], B_ring[r]))
            ws["pos"] += 1
        return outl

    def skip_blocks(n):
        ws["pos"] += n

    def stats_chunk(c, N, pst, Bpst):
        t, Bt = tmpf()
        tb16 = t[:, :].bitcast(BF16)
        act(tb16[:, 0:N], xT[:, c, 0:N], AF.Square, (B_xT[c],), (Bt,))
        mm(pst[:, 0:N], onesb[:], tb16[:, 0:N], c == 0, c == 7, (Bt, B_ones), (Bpst,))

    def get_rstd(N):
        if stats["ready"] is None:
            pst, Bpst = psum_fixed(6)
            for c in range(8):
                stats_chunk(c, N, pst, Bpst)
        else:
            pst, Bpst = stats["ready"]
            stats["ready"] = None
        t1, Bt1 = tmpf()
        act(t1[:, 0:N], pst[:, 0:N], AF.Ln, (Bpst,), (Bt1,), bias=EPS, scale=1.0 / D)
        act(rstd_sb[:, 0:N], t1[:, 0:N], AF.Exp, (Bt1,), (B_rstd,), scale=-0.5)

    def rmsnorm(N, gcol):
        P.tag = "norm"
        get_rstd(N)
        for c in range(8):
            stt(xn[:, c, 0:N], xT[:, c, 0:N], smp[:, gcol + c:gcol + c + 1], rstd_sb[:, 0:N], ALU.mult, ALU.mult,
                (B_xT[c], B_rstd, B_smp), (B_xn[c],))

    def ffn(N, gcol):
        rmsnorm(N, gcol)
        Bg = retarget(B_G, "g")
        P.tag = "ffn_in"
        g = arenaG[:, 0:NF * TN].rearrange("p (f t) -> p f t", f=NF)
        for i in range(NF):
            (wa, Bwa), (wb, Bwb) = next_blocks(2)
            pa, Bpa = psum()
            pb, Bpb = psum()
            for kc in range(8):
                mm(pa[:, 0:N], wa[:, kc * 128:(kc + 1) * 128], xn[:, kc, 0:N], kc == 0, kc == 7, (Bwa, B_xn[kc]), (Bpa,))
            for kc in range(8):
                mm(pb[:, 0:N], wb[:, kc * 128:(kc + 1) * 128], xn[:, kc, 0:N], kc == 0, kc == 7, (Bwb, B_xn[kc]), (Bpb,))
            t, Bt = tmpf()
            act(t[:, 0:N], pa[:, 0:N], AF.Silu, (Bpa,), (Bt,))
            tt("dve", g[:, i, 0:N], t[:, 0:N], pb[:, 0:N], ALU.mult, (Bt, Bpb), (Bg,))
        pos_ = []
        P.tag = "ffn_out"
        for dj in range(8):
            pos_.append(psum_fixed(dj))
        TAIL = 4
        for i in range(NF - TAIL):
            (wo, Bwo), = next_blocks(1)
            for dj in range(8):
                po, Bpo = pos_[dj]
                mm(po[:, 0:N], wo[:, dj * 128:(dj + 1) * 128], g[:, i, 0:N], i == 0, False, (Bwo, Bg), (Bpo,))
        tail = next_blocks(TAIL)
        for dj in range(8):
            po, Bpo = pos_[dj]
            for k_, (wo, Bwo) in enumerate(tail):
                i = NF - TAIL + k_
                mm(po[:, 0:N], wo[:, dj * 128:(dj + 1) * 128], g[:, i, 0:N], False, i == NF - 1, (Bwo, Bg), (Bpo,))
        P.tag = "ffn_evac"
        pst, Bpst = psum_fixed(0)
        for dj in range(8):
            po, Bpo = pos_[dj]
            stt(xT[:, dj, 0:N], po[:, 0:N], 0.5, xT[:, dj, 0:N], ALU.mult, ALU.add, (Bpo, B_xT[dj]), (B_xT[dj],))
            stats_chunk(dj, N, pst, Bpst)
        stats["ready"] = (pst, Bpst)

    def rope_tables(N, pos_off):
        ch = ch_pos
        dma("sp", posb[:, 0:N], mk_ap(posd, pos_off, [[0, 128], [1, N]]), ch, (), (B_posb,))
        ang, Ba = tmpf()
        ts("dve", ang[:, 0:N], posb[:, 0:N], freq[:, 0:1], None, ALU.mult, None, (B_posb, B_freq), (Ba,))
        for which in range(2):
            src, Bs = ang, Ba
            if which == 1:
                a2, Ba2 = tmpf()
                ts("dve", a2[:, 0:N], ang[:, 0:N], math.pi / 2.0, None, ALU.add, None, (Ba,), (Ba2,))
                src, Bs = a2, Ba2
            ts("dve", tmpi[:, 0:N], src[:, 0:N], 1.0 / TWO_PI, None, ALU.mult, None, (Bs,), (B_tmpi,))
            kf, Bk = tmpf()
            cp("dve", kf[:, 0:N], tmpi[:, 0:N], (B_tmpi,), (Bk,))
            r, Br = tmpf()
            stt(r[:, 0:N], kf[:, 0:N], -TWO_PI, src[:, 0:N], ALU.mult, ALU.add, (Bk, Bs), (Br,))
            ts("dve", r[:, 0:N], r[:, 0:N], -math.pi, math.pi, ALU.max, ALU.min, (Br,), (Br,))
            dst = sinT if which == 0 else cosT
            act(dst[:, 0:N], r[:, 0:N], AF.Sin, (Br,), (B_rope,))
        ts("dve", cosT[:, 0:N], cosT[:, 0:N], smp[:, C_RM:C_RM + 1], smp[:, C_RO:C_RO + 1], ALU.mult, ALU.add,
           (B_rope, B_smp), (B_rope,))
        ts("dve", sinT[:, 0:N], sinT[:, 0:N], smp[:, C_RS:C_RS + 1], None, ALU.mult, None, (B_rope, B_smp), (B_rope,))

    def proj_fm(N, w, Bw):
        pz, Bz = psum()
        for kc in range(8):
            mm(pz[:, 0:N], w[:, kc * 128:(kc + 1) * 128], xn[:, kc, 0:N], kc == 0, kc == 7, (Bw, B_xn[kc]), (Bz,))
        return pz, Bz

    def rope_block(N, dst_ap, Bdst, f32_dst=None, Bf32=None):
        (w, Bw), (w2, Bw2) = next_blocks(2)
        pz, Bz = proj_fm(N, w, Bw)
        pz2, Bz2 = proj_fm(N, w2, Bw2)
        t1, Bt1 = tmpf()
        t2, Bt2 = tmpf()
        tt("dve", t1[:, 0:N], pz[:, 0:N], cosT[:, 0:N], ALU.mult, (Bz, B_rope), (Bt1,))
        tt("dve", t2[:, 0:N], pz2[:, 0:N], sinT[:, 0:N], ALU.mult, (Bz2, B_rope), (Bt2,))
        if f32_dst is None:
            tt("pool", dst_ap, t1[:, 0:N], t2[:, 0:N], ALU.add, (Bt1, Bt2), (Bdst,))
        else:
            tt("pool", f32_dst[:, 0:N], t1[:, 0:N], t2[:, 0:N], ALU.add, (Bt1, Bt2), (Bf32,))
            cp("act", dst_ap, f32_dst[:, 0:N], (Bf32,), (Bdst,))

    def transpose_out(N, src_f32, Bsrc, dst_stage, Bdst, col0, ncols=128, rows=128):
        nb = N // 128
        pt, Bp = psum()
        for tb in range(nb):
            tr(pt[:, tb * 128: tb * 128 + rows], src_f32[0:rows, tb * 128:(tb + 1) * 128], ident[0:rows, 0:rows],
               (Bsrc, B_ident), (Bp,))
        cp("dve", dst_stage[:, 0:nb, col0:col0 + rows],
           pt[:, 0:nb * 128].rearrange("p (b c) -> p b c", b=nb)[:, :, 0:rows], (Bp,), (Bdst,))

    def mixer(N, l, tinfo):
        kind = tinfo["kind"]
        is_out = tinfo["out"] and not _CACHE.get("no_out", False)
        nb = N // 128
        rmsnorm(N, C_NORM + (l * 3 + 1) * 8)
        retarget(B_G, "mix")
        Bq = Buf("qy")
        Bq.r = list(B_G["cur"].r)
        Bqa, Bqb, Bya, Byb = Buf("qa"), Buf("qb"), Buf("ya"), Buf("yb")
        for b_ in (Bqa, Bqb, Bya, Byb):
            b_.r = list(B_G["cur"].r)
        qaT = arenaG[:, 0 * TN:4 * TN].rearrange("p (j t) -> p j t", j=4)
        qbT = arenaG[:, 4 * TN:8 * TN].rearrange("p (j t) -> p j t", j=4)
        yaT = arenaG[:, 8 * TN:12 * TN].rearrange("p (j t) -> p j t", j=4)
        ybT = arenaG[:, 12 * TN:16 * TN].rearrange("p (j t) -> p j t", j=4)
        Bstg = Buf("stg")
        Bstg.r = list(B_G["cur"].r)
        if kind == "prompt":
            t = tinfo["t"]
            rh = (t % 2) * 512
            rb = (t % 2) * 4
            kb_dst = lambda j: kbT[l][:, j, rh:rh + N]
            ka_dst = kaT[l][:, rh:rh + N]
            Bkb, Bka, Bvb, Bva = B_kbT[l], B_kaT[l], B_vbr[l], B_var[l]
            vb_dst = lambda tb: vbr[l][:, rb + tb, :]
            va_dst = var_[l][:, rb:rb + nb, :]
        else:
            kb_dst = lambda j: knewb[:, j, 0:N]
            ka_dst = knewa[:, 0:N]
            Bkb, Bka, Bvb, Bva = B_knb, B_kna, B_vnb, B_vna
            vb_dst = lambda tb: vnewb[:, tb, :]
            va_dst = vnewa[:, 0:nb, :]

        kv_only = (tinfo.get("mode") == "KV")
        P.tag = "mix_proj"
        if kv_only:
            skip_blocks(8)
        else:
            for j in range(4):
                rope_block(N, qaT[:, j, 0:N], Bqa)
        kaf, Bkaf = tmpf()
        rope_block(N, ka_dst, Bka, f32_dst=kaf, Bf32=Bkaf)
        if is_out:
            transpose_out(N, kaf, Bkaf, kaout, B_kaout, 0)
        wv = next_blocks(4)
        wv_ap = wv[0][0]
        Bwv = wv[0][1]
        r_ = ws["cur"] % NRING
        wv3 = wring[:, r_, :].rearrange("p (k c) -> p k c", k=8)
        if is_out:
            Bvout = retarget(B_U, "vout")
            vout = arenaU[:, :].bitcast(F32).rearrange("p (b c) -> p b c", b=4)
        for tb in range(nb):
            pv, Bpv = psum()
            for kc in range(8):
                mm(pv[:, 0:512], xn[:, kc, tb * 128:(tb + 1) * 128], wv3[:, kc, :], kc == 0, kc == 7, (Bwv, B_xn[kc]), (Bpv,))
            cp("act", vb_dst(tb), pv[:, 0:512], (Bpv,), (Bvb,))
            if is_out:
                cp("dve", vout[:, tb, :], pv[:, 0:512], (Bpv,), (Bvout,))
        if is_out:
            store_v_band(l, tinfo, vout, Bvout, nb)
        (wva, Bwva), = next_blocks(1)
        pv, Bpv = psum()
        for tb in range(nb):
            for kc in range(8):
                mm(pv[:, tb * 128:(tb + 1) * 128], xn[:, kc, tb * 128:(tb + 1) * 128], wva[:, kc * 128:(kc + 1) * 128],
                   kc == 0, kc == 7, (Bwva, B_xn[kc]), (Bpv,))
        pv3 = pv[:, 0:nb * 128].rearrange("p (b c) -> p b c", b=nb)
        cp("act", va_dst, pv3, (Bpv,), (Bva,))
        if is_out:
            cp("dve", vaout[:, 0:nb, :], pv3, (Bpv,), (B_vaout,))
        for j in range(4):
            if kv_only:
                skip_blocks(1)
                continue
            (w, Bw), = next_blocks(1)
            pz, Bz = proj_fm(N, w, Bw)
            cp("act", qbT[:, j, 0:N], pz[:, 0:N], (Bz,), (Bqb,))
        if is_out:
            Bkout = retarget(B_U, "kout")
            kout = arenaU[:, :].bitcast(F32).rearrange("p (b c) -> p b c", b=4)
        for j in range(4):
            (w, Bw), = next_blocks(1)
            pz, Bz = proj_fm(N, w, Bw)
            cp("act", kb_dst(j), pz[:, 0:N], (Bz,), (Bkb,))
            if is_out:
                kf, Bkf = tmpf()
                cp("dve", kf[:, 0:N], pz[:, 0:N], (Bz,), (Bkf,))
                transpose_out(N, kf, Bkf, kout, Bkout, j * 128)
        if is_out:
            store_k_band(l, tinfo, kout, Bkout, nb)
            store_win(l, tinfo, nb)


        if kv_only:
            skip_blocks(32)
            cur = B_G["cur"]
            for b_ in (Bqa, Bqb, Bya, Byb, Bstg):
                cur.r.extend(b_.r)
                if b_.w is not None:
                    cur.r.append(b_.w)
            return
        if stop_after == "mixproj":
            return
        all_units = []
        if kind == "prompt":
            t = tinfo["t"]
            for p in range(nb):
                gB = t * 4 + p
                keysB = []
                for i_, kb_ in enumerate(range(gB - 4, gB + 1)):
                    rbk = kb_ % 8
                    keysB.append(dict(kT=lambda rows, hp, rbk=rbk: kbT[l][rows, hp, rbk * 128:(rbk + 1) * 128],
                                      v=lambda h, rbk=rbk: vbr[l][:, rbk, h * 64:(h + 1) * 64],
                                      vld=vldr[:, rbk, :], nk=128, base=0, kind=i_, Bk=B_kbT[l], Bv=B_vbr[l],
                                      Bvld=B_vld))
                keysA = []
                for i_, kb_ in enumerate(range(gB - 1, gB + 1)):
                    rbk = kb_ % 8
                    keysA.append(dict(kT=lambda rows, rbk=rbk: kaT[l][rows, rbk * 128:(rbk + 1) * 128],
                                      v=lambda kv, rbk=rbk: var_[l][:, rbk, kv * 64:(kv + 1) * 64],
                                      vld=vldr[:, rbk, :], nk=128, base=0, kind=i_, Bk=B_kaT[l], Bv=B_var[l],
                                      Bvld=B_vld))
                all_units.extend(attend(l, p * 128, 128, keysA, keysB, qaT, qbT, yaT, ybT, Bqa, Bqb, Bya, Byb))
        else:
            for s in range(NS):
                rs = s % 2
                load_cache(l, s, rs, Bstg)
                base = (s % 2) * 64
                tbk = s // 2
                keysB = []
                for i_ in range(4):
                    keysB.append(dict(kT=lambda rows, hp, i_=i_: kbT[rs][rows, hp, i_ * 128:(i_ + 1) * 128],
                                      v=lambda h, i_=i_: vbr[rs][:, i_, h * 64:(h + 1) * 64],
                                      vld=ones64[:, :], nk=128, base=0, kind=i_, Bk=B_kbT[rs], Bv=B_vbr[rs],
                                      Bvld=B_ones))
                keysB.append(dict(kT=lambda rows, hp: knewb[rows, hp, s * 64:(s + 1) * 64],
                                  v=lambda h: vnewb[base:base + 64, tbk, h * 64:(h + 1) * 64],
                                  vld=ones64[base:base + 64, :], nk=64, base=base, kind=4, Bk=B_knb, Bv=B_vnb,
                                  Bvld=B_ones))
                keysA = [dict(kT=lambda rows: kaT[rs][rows, 0:128],
                              v=lambda kv: var_[rs][:, 0, kv * 64:(kv + 1) * 64],
                              vld=ones64[:, :], nk=128, base=0, kind=0, Bk=B_kaT[rs], Bv=B_var[rs], Bvld=B_ones),
                         dict(kT=lambda rows: knewa[rows, s * 64:(s + 1) * 64],
                              v=lambda kv: vnewa[base:base + 64, tbk, kv * 64:(kv + 1) * 64],
                              vld=ones64[base:base + 64, :], nk=64, base=base, kind=1, Bk=B_kna, Bv=B_vna,
                              Bvld=B_ones)]
                run_units(attend(l, s * 64, 64, keysA, keysB, qaT, qbT, yaT, ybT, Bqa, Bqb, Bya, Byb))

        P.tag = "attn"
        run_units(all_units)
        P.tag = "merge"
        if stop_after == "attn":
            return
        Bu = retarget(B_U, "u")
        u = arenaU[:, 0:8 * TN].rearrange("p (c t) -> p c t", c=8)
        for j in range(8):
            (wga, Bwga), (wgb, Bwgb), (wbr, Bwbr) = next_blocks(3)
            pga, Bpga = proj_fm(N, wga, Bwga)
            pgb, Bpgb = proj_fm(N, wgb, Bwgb)
            ga, Bga = tmpf()
            gb, Bgb = tmpf()
            act(ga[:, 0:N], pga[:, 0:N], AF.Sigmoid, (Bpga, B_smp), (Bga,),
                bias=smp[:, C_BG + l * 16 + j:C_BG + l * 16 + j + 1])
            act(gb[:, 0:N], pgb[:, 0:N], AF.Sigmoid, (Bpgb, B_smp), (Bgb,),
                bias=smp[:, C_BG + l * 16 + 8 + j:C_BG + l * 16 + 8 + j + 1])
            pua, Bpua = psum()
            pub, Bpub = psum()
            for kc in range(4):
                mm(pua[:, 0:N], wbr[:, kc * 128:(kc + 1) * 128], yaT[:, kc, 0:N], kc == 0, kc == 3, (Bwbr, Bya), (Bpua,))
            for kc in range(4):
                mm(pub[:, 0:N], wbr[:, 512 + kc * 128:512 + (kc + 1) * 128], ybT[:, kc, 0:N], kc == 0, kc == 3,
                   (Bwbr, Byb), (Bpub,))
            tt("dve", ga[:, 0:N], pua[:, 0:N], ga[:, 0:N], ALU.mult, (Bpua, Bga), (Bga,))
            tt("dve", gb[:, 0:N], pub[:, 0:N], gb[:, 0:N], ALU.mult, (Bpub, Bgb), (Bgb,))
            tt("pool", u[:, j, 0:N], ga[:, 0:N], gb[:, 0:N], ALU.add, (Bga, Bgb), (Bu,))
        P.tag = "mix_out"
        pst, Bpst = psum_fixed(6)
        for dj in range(8):
            (w, Bw), = next_blocks(1)
            po, Bpo = psum()
            for kc in range(8):
                mm(po[:, 0:N], w[:, kc * 128:(kc + 1) * 128], u[:, kc, 0:N], kc == 0, kc == 7, (Bw, Bu), (Bpo,))
            tt("dve", xT[:, dj, 0:N], po[:, 0:N], xT[:, dj, 0:N], ALU.add, (Bpo, B_xT[dj]), (B_xT[dj],))
            if dj >= 1:
                stats_chunk(dj - 1, N, pst, Bpst)
        stats_chunk(7, N, pst, Bpst)
        stats["ready"] = (pst, Bpst)
        cur = B_G["cur"]
        for b_ in (Bqa, Bqb, Bya, Byb, Bstg):
            cur.r.extend(b_.r)
            if b_.w is not None:
                cur.r.append(b_.w)

    def attend(l, q0, nq, keysA, keysB, qaT, qbT, yaT, ybT, Bqa, Bqb, Bya, Byb):
        units = []
        fast = (nq == 128)
        pO, BpO = psum_fixed(4)
        pD, BpD = psum_fixed(5)
        nkb = len(keysB)
        pslot = {0: 0, 3: 1, 4: 2, 1: 3, 2: 4}
        slot = {0: (0, 0), 3: (0, 1), 4: (0, 2), 1: (1, 0), 2: (1, 1)}

        def make_b(h):
            hp, half = h // 2, h % 2
            rows = slice(half * 64, half * 64 + 64)
            stt_ = {}

            def qk():
                pS, BpS = psum()
                pS2, BpS2 = psum()
                pi = st["ptb"] % 4
                st["ptb"] += 1
                pT, BpT = pTb[pi], B_pTb[pi]
                stt_["pT"] = (pT, BpT)
                dsts = []
                if fast:
                    mm(pS[:, 0:384], identb[:, :], tb3[l][:, h, :, :].rearrange("p b c -> p (b c)"), True, False,
                       (B_ones, B_t34), (BpS,))
                for i_, kb_ in enumerate(keysB):
                    nk, base = kb_["nk"], kb_["base"]
                    bank, sl_ = slot[kb_["kind"]]
                    if bank == 0:
                        dst = pS[base:base + nk, sl_ * 128:sl_ * 128 + nq]
                        Bd = BpS
                    else:
                        dst = pS2[base:base + nk, sl_ * 128:sl_ * 128 + nq]
                        Bd = BpS2
                    if fast and bank == 0:
                        mm(dst, kb_["kT"](rows, hp), qbT[rows, hp, q0:q0 + nq], False, kb_["kind"] == 4,
                           (kb_["Bk"], Bqb), (Bd,))
                    else:
                        mm(dst, kb_["kT"](rows, hp), qbT[rows, hp, q0:q0 + nq], True, True, (kb_["Bk"], Bqb), (Bd,))
                    dsts.append((dst, Bd))
                bcol = smp[:, C_BC + l * 8 + h:C_BC + l * 8 + h + 1]
                if fast:
                    act(pT[:, 3:5, :], pS2[:, 0:256].rearrange("p (b c) -> p b c", b=2), AF.Exp, (BpS2, B_smp), (BpT,),
                        bias=bcol, scale=0.125)
                    act(pT[:, 0:3, :], pS[:, 0:384].rearrange("p (b c) -> p b c", b=3), AF.Exp, (BpS,), (BpT,),
                        scale=0.125)
                    return
                for i_, kb_ in enumerate(keysB):
                    nk, base = kb_["nk"], kb_["base"]
                    dst, Bd = dsts[i_]
                    pr = slice(base, base + nk)
                    kd = kb_["kind"]
                    ps_ = pslot[kd]
                    if kd in (1, 2):
                        act(pT[pr, ps_, 0:nq], dst, AF.Exp, (Bd, B_smp), (BpT,), bias=bcol[pr, :], scale=0.125)
                    else:
                        tm, Btm = tmpf()
                        bi = {0: 0, 3: 1, 4: 2}[kd]
                        btile = tb3[l][0:nk, h, bi, 0:nq] if base == 0 else t34_hi[l][pr, 0, h, 0:nq]
                        tt("dve", tm[pr, 0:nq], dst, btile, ALU.add, (Bd, B_t34), (Btm,))
                        act(pT[pr, ps_, 0:nq], tm[pr, 0:nq], AF.Exp, (Btm,), (BpT,), scale=0.125)

            def pv():
                pT, BpT = stt_["pT"]
                for i_, kb_ in enumerate(keysB):
                    nk, base = kb_["nk"], kb_["base"]
                    pr = slice(base, base + nk)
                    mm(pO[rows, hp * 128:hp * 128 + nq], kb_["v"](h), pT[pr, pslot[kb_["kind"]], 0:nq], i_ == 0,
                       i_ == nkb - 1, (kb_["Bv"], BpT), (BpO,))
                for i_, kb_ in enumerate(keysB):
                    nk, base = kb_["nk"], kb_["base"]
                    pr = slice(base, base + nk)
                    mm(pD[rows, hp * 128:hp * 128 + nq], kb_["vld"], pT[pr, pslot[kb_["kind"]], 0:nq], i_ == 0,
                       i_ == nkb - 1, (kb_["Bvld"], BpT), (BpD,))
            return qk, pv

        def post_b():
            pO3 = pO[:, :].rearrange("p (j c) -> p j c", j=4)[:, :, 0:nq]
            pD3 = pD[:, :].rearrange("p (j c) -> p j c", j=4)[:, :, 0:nq]
            rc, Brc = tmpf()
            rc3 = rc[:, :].rearrange("p (j c) -> p j c", j=4)[:, :, 0:nq]
            ts("dve", rc3, pD3, 1e-30, None, ALU.add, None, (BpD,), (Brc,))
            recip(rc3, rc3, (Brc,), (Brc,))
            tt("dve", ybT[:, :, q0:q0 + nq], pO3, rc3, ALU.mult, (BpO, Brc), (Byb,))

        for h in range(8):
            qk, pv = make_b(h)
            units.append((qk, pv, post_b if h == 7 else None))

        pOa, BpOa = psum_fixed(6)
        pDa, BpDa = psum_fixed(7)
        nka = len(keysA)

        def make_a(kv):
            rows = slice(kv * 64, kv * 64 + 64)
            stt_ = {}

            def qk():
                pi = st["pta"] % 3
                st["pta"] += 1
                pT, BpT = pTa[pi], B_pTa[pi]
                stt_["pT"] = (pT, BpT)
                dsts = []
                for i_, ka_ in enumerate(keysA):
                    nk, base = ka_["nk"], ka_["base"]
                    pS, BpS = psum()
                    pS3 = pS[:, 0:4 * nq].rearrange("p (j c) -> p j c", j=4)
                    dst = pS3[base:base + nk, :, :]
                    mm(dst, ka_["kT"](rows), qaT[rows, :, q0:q0 + nq], True, True, (ka_["Bk"], Bqa), (BpS,))
                    dsts.append((dst, BpS))
                for i_, ka_ in enumerate(keysA):
                    nk, base = ka_["nk"], ka_["base"]
                    pr = slice(base, base + nk)
                    dst, BpS = dsts[i_]
                    act(pT[pr, i_, :, 0:nq], dst, AF.Exp, (BpS,), (BpT,), scale=0.125)
                    if nq == 128:
                        if ka_["kind"] == 0:
                            memset("pool", pT[0:64, i_, :, 64:128], 0.0, (BpT,))
                        else:
                            memset("pool", pT[64:128, i_, :, 0:64], 0.0, (BpT,))

            def pv():
                pT, BpT = stt_["pT"]
                for i_, ka_ in enumerate(keysA):
                    nk, base = ka_["nk"], ka_["base"]
                    pr = slice(base, base + nk)
                    mm(pOa[rows, 0:4 * nq].rearrange("p (j c) -> p j c", j=4), ka_["v"](kv), pT[pr, i_, :, 0:nq],
                       i_ == 0, i_ == nka - 1, (ka_["Bv"], BpT), (BpOa,))
                for i_, ka_ in enumerate(keysA):
                    nk, base = ka_["nk"], ka_["base"]
                    pr = slice(base, base + nk)
                    mm(pDa[rows, 0:4 * nq].rearrange("p (j c) -> p j c", j=4), ka_["vld"], pT[pr, i_, :, 0:nq],
                       i_ == 0, i_ == nka - 1, (ka_["Bvld"], BpT), (BpDa,))
            return qk, pv

        def post_a():
            pO3 = pOa[:, 0:4 * nq].rearrange("p (j c) -> p j c", j=4)
            pD3 = pDa[:, 0:4 * nq].rearrange("p (j c) -> p j c", j=4)
            rc, Brc = tmpf()
            rc3 = rc[:, :].rearrange("p (j c) -> p j c", j=4)[:, :, 0:nq]
            es3 = mk_ap(esink, esink[:, l * 4:l * 4 + 4].offset, [[8, 128], [1, 4], [0, nq]])
            tt("dve", rc3, pD3, es3, ALU.add, (BpDa, B_esink), (Brc,))
            recip(rc3, rc3, (Brc,), (Brc,))
            tt("dve", yaT[:, :, q0:q0 + nq], pO3, rc3, ALU.mult, (BpOa, Brc), (Bya,))

        for kv in range(2):
            qk, pv = make_a(kv)
            units.append((qk, pv, post_a if kv == 1 else None))
        return units

    def run_units(units, depth=1):
        n = len(units)
        for i in range(n + depth):
            if i < n:
                units[i][0]()
            j = i - depth
            if j >= 0:
                units[j][1]()
                if units[j][2] is not None:
                    units[j][2]()

    ch_kout = P.chan("kout")
    ch_vout = P.chan("vout")
    ch_kaout = P.chan("kaout")
    ch_vaout = P.chan("vaout")

    def store_k_band(l, tinfo, kout, Bk, nb):
        if tinfo["kind"] == "prompt":
            dma("sp", kbp_o[l].rearrange("(b p) c -> p b c", p=128), kout[:, 0:4, :], ch_kout, (Bk,), ())
        else:
            for s in range(NS):
                base, tbk = (s % 2) * 64, s // 2
                dma("sp", kbs_o[l, s, 448:512, :], kout[base:base + 64, tbk, :], ch_kout, (Bk,), ())

    def store_v_band(l, tinfo, vout, Bv, nb):
        if tinfo["kind"] == "prompt":
            dma("sp", vbp_o[l].rearrange("(b p) c -> p b c", p=128), vout[:, 0:4, :], ch_vout, (Bv,), ())
        else:
            for s in range(NS):
                base, tbk = (s % 2) * 64, s // 2
                dma("sp", vbs_o[l, s, 448:512, :], vout[base:base + 64, tbk, :], ch_vout, (Bv,), ())

    def store_win(l, tinfo, nb):
        if tinfo["kind"] == "prompt":
            dma("sp", kwp_o[l], kaout[:, 3, :], ch_kaout, (B_kaout,), ())
            dma("sp", vwp_o[l], vaout[:, 3, :], ch_vaout, (B_vaout,), ())
        else:
            for s in range(NS):
                base, tbk = (s % 2) * 64, s // 2
                dma("sp", kws_o[l, s, 64:128, :], kaout[base:base + 64, tbk, :], ch_kaout, (B_kaout,), ())
                dma("sp", vws_o[l, s, 64:128, :], vaout[base:base + 64, tbk, :], ch_vaout, (B_vaout,), ())

    knewb = sb("knewb", [128, 4, SN], BF16)
    knewa = sb("knewa", [128, SN], BF16)
    vnewb = sb("vnewb", [128, 2, 512], BF16)
    vnewa = sb("vnewa", [128, 2, 128], BF16)
    zmask = sb("zmask", [128, 128], F32)
    t34_hi = [sb("t34hi_%d" % l, [128, 1, 8, 64], BF16) for l in range(2)]
    B_knb, B_kna, B_vnb, B_vna = Buf("knb"), Buf("kna"), Buf("vnb"), Buf("vna")
    memset("pool", zmask[:], 0.0, (B_masks,))
    def build_toeplitz_hi():
        toep_load(256, 0, 64, 64, lambda l: t34_hi[l][64:128, 0, :, :])

    ch_cvb = [P.chan("cvb%d" % i) for i in range(2)]
    ch_cva = [P.chan("cva%d" % i) for i in range(2)]
    ch_ckb = P.chan("ckbst")
    ch_cka = P.chan("ckast")

    def load_cache(l, s, rs, Bstg):
        dma("pool", vbr[rs][:, 0:4, :], cvb[l, s].rearrange("(b p) c -> p b c", p=128), ch_cvb[rs], (), (B_vbr[rs],))
        dma("pool", var_[rs][:, 0, :], cvw[l, s], ch_cva[rs], (), (B_var[rs],))
        for half in range(2):
            kc2 = arenaG[:, 16 * TN:16 * TN + 2 * 2 * 512].bitcast(F32).rearrange("p (b c) -> p b c", b=2)
            dma("sp", kc2, ckb[l, s, half * 256:(half + 1) * 256, :].rearrange("(b p) c -> p b c", p=128),
                ch_ckb, (), (Bstg,))
            for hp in range(4):
                pt, Bp = psum()
                for tb in range(2):
                    tr(pt[:, tb * 128:(tb + 1) * 128], kc2[:, tb, hp * 128:(hp + 1) * 128], ident[:, :],
                       (Bstg, B_ident), (Bp,))
                cp("act", kbT[rs][:, hp, half * 256:(half + 1) * 256], pt[:, 0:256], (Bp,), (B_kbT[rs],))
        dma("sp", kcs_a[:], ckw[l, s], ch_cka, (), (B_kcsa,))
        pt, Bp = psum()
        tr(pt[:, 0:128], kcs_a[:, :], ident[:, :], (B_kcsa, B_ident), (Bp,))
        cp("act", kaT[rs][:, 0:128], pt[:, 0:128], (Bp,), (B_kaT[rs],))

    ch_x = P.chan("xin")
    ch_x2 = P.chan("xin2")
    ch_pos = P.chan("pos")
    ch_vld = P.chan("vldin")
    ch_y = P.chan("yout")

    xst_views = [arenaU[:, :].bitcast(F32).rearrange("p (b c) -> p b c", b=2),
                 xn[:, :, :].rearrange("p c t -> p (c t)").bitcast(F32).rearrange("p (b c) -> p b c", b=2)]
    xpre = {"bufs": None}

    def issue_x_dma(N, src_rows_ap):
        nb = N // 128
        P.tag = "load_x"
        BstA = retarget(B_U, "xstageA")
        dma("sp", xst_views[0], src_rows_ap[0:256, :].rearrange("(b p) c -> p b c", p=128), ch_x, (), (BstA,))
        if nb > 2:
            dma("sp", xst_views[1], src_rows_ap[256:512, :].rearrange("(b p) c -> p b c", p=128), ch_x2, (),
                tuple(B_xn))
        xpre["bufs"] = BstA

    def load_x(N, src_rows_ap):
        nb = N // 128
        if xpre["bufs"] is None:
            issue_x_dma(N, src_rows_ap)
        BstA = xpre["bufs"]
        xpre["bufs"] = None
        P.tag = "load_x"
        for c in range(8):
            pt, Bp = psum()
            for tb in range(nb):
                if tb < 2:
                    tr(pt[:, tb * 128:(tb + 1) * 128], xst_views[0][:, tb, c * 128:(c + 1) * 128], ident[:, :],
                       (BstA, B_ident), (Bp,))
                else:
                    tr(pt[:, tb * 128:(tb + 1) * 128], xst_views[1][:, tb - 2, c * 128:(c + 1) * 128], ident[:, :],
                       tuple(B_xn) + (B_ident,), (Bp,))
            if c % 2 == 0:
                cp("act", xT[:, c, 0:N], pt[:, 0:N], (Bp,), (B_xT[c],))
            else:
                cp("dve", xT[:, c, 0:N], pt[:, 0:N], (Bp,), (B_xT[c],))

    def final_out(N, dst_rows_ap):
        nb = N // 128
        P.tag = "final"
        Bst = retarget(B_G, "ystage")
        yst = arenaG[:, 0:nb * 2 * D].bitcast(F32).rearrange("p (b c) -> p b c", b=nb)
        get_rstd(N)
        gcol = C_NORM + 6 * 8
        for c in range(8):
            yn, Byn = tmpf()
            stt(yn[:, 0:N], xT[:, c, 0:N], smp[:, gcol + c:gcol + c + 1], rstd_sb[:, 0:N], ALU.mult, ALU.mult,
                (B_xT[c], B_rstd, B_smp), (Byn,))
            ptr, Bptr = psum()
            for tb in range(nb):
                tr(ptr[:, tb * 128:(tb + 1) * 128], yn[:, tb * 128:(tb + 1) * 128], ident[:, :], (Byn, B_ident), (Bptr,))
            eng = "act" if c % 2 == 0 else "dve"
            cp(eng, yst[:, :, c * 128:(c + 1) * 128], ptr[:, 0:N].rearrange("p (b c) -> p b c", b=nb), (Bptr,), (Bst,))
        dma("sp", dst_rows_ap.rearrange("(b p) c -> p b c", p=128), yst, ch_y, (Bst,), ())

    def run_tile(N, tinfo, x_rows_ap, pos_off, y_rows_ap, next_x=None):
        stats["ready"] = None
        load_x(N, x_rows_ap)
        if stop_after == "load":
            return
        rope_tables(N, pos_off)
        if stop_after == "rope":
            return
        if tinfo["kind"] == "prompt":
            t = tinfo["t"]
            rb = (t % 2) * 4
            dma("pool", vldr[:, rb:rb + 4, :], vldd[t * TN:(t + 1) * TN, :].rearrange("(b p) c -> p b c", p=128),
                ch_vld, (), (B_vld,))
        modes = tile_modes(tinfo["kind"], tinfo["t"])
        for l in range(n_layers):
            mode = modes[l]
            if mode == "NONE":
                continue
            ws["pos"] = 0
            ws["lbase"] = l * NSLOT_L
            tinfo["mode"] = mode
            ffn(N, C_NORM + (l * 3 + 0) * 8)
            assert ws["pos"] == 66
            if stop_after == "ffn1":
                return
            mixer(N, l, tinfo)
            if stop_after in ("mixer", "mixproj", "attn"):
                return
            assert ws["pos"] == 121, ws["pos"]
            if mode == "FULL":
                ffn(N, C_NORM + (l * 3 + 2) * 8)
                assert ws["pos"] == 187
        if next_x is not None:
            issue_x_dma(next_x[0], next_x[1])
        if y_rows_ap is not None:
            final_out(N, y_rows_ap)

    if n_ptiles > 0:
        issue_x_dma(TN, xseg[0:TN, :])
    elif with_sample:
        issue_x_dma(SN, xsmp[:, :])
    for _ in range(NRING - 1):
        issue_slot()
    toep_done = {"v": False}

    def ensure_toeplitz():
        if not toep_done["v"]:
            toep_done["v"] = True
            build_toeplitz()
            build_toeplitz_hi()

    for t in range(n_ptiles):
        if t >= 1:
            ensure_toeplitz()
        is_last = (t == n_ptiles - 1)
        yr = y_o[(t - 2) * TN:(t - 1) * TN, :] if t >= 2 else None
        if t + 1 < n_ptiles:
            nx = (TN, xseg[(t + 1) * TN:(t + 2) * TN, :])
        elif with_sample:
            nx = (SN, xsmp[:, :])
        else:
            nx = None
        if stop_after is not None:
            nx = None
        run_tile(TN, dict(kind="prompt", t=t, out=is_last), xseg[t * TN:(t + 1) * TN, :], t * TN, yr, next_x=nx)
    if with_sample:
        ensure_toeplitz()
        run_tile(SN, dict(kind="sample", t=None, out=True), xsmp[:, :], SEG, ys_o[:, :])

    counts, nwait = P.emit()
    _CACHE["pe_tags"] = [e.tag for e in P.entries if e.eng == "pe"]
    _CACHE["tags"] = {k: [e.tag for e in P.entries if e.eng == k] for k in ("act", "dve")}
    return nc, counts, nwait, len(P.entries)


def _f1(W, cols):
    K = W.shape[0]
    blk = W[:, cols].reshape(K // 128, 128, len(cols)).transpose(1, 0, 2)
    return np.ascontiguousarray(blk).reshape(128, -1)


def _weight_stream(inp):
    blocks = np.zeros((2, NBLK, 128, 1024), np.float32)
    sw = np.concatenate([np.arange(8, 16), np.arange(0, 8), np.arange(16, 64)])
    for l in range(2):
        b = 0
        def put(arr):
            nonlocal b
            blocks[l, b] = arr
            b += 1
        def ffn_blocks(win, wout):
            for i in range(NF):
                put(_f1(win, np.arange(i * 128, (i + 1) * 128)))
                put(_f1(win, np.arange(DFF + i * 128, DFF + (i + 1) * 128)))
            for i in range(NF):
                put(wout[i * 128:(i + 1) * 128, :])
        ffn_blocks(inp["w_ffn1_in"][l], inp["w_ffn1_out"][l])
        W = inp["w_mix_in"][l]
        for j in range(4):
            c = np.concatenate([np.arange(j * 64, (j + 1) * 64), np.arange((4 + j) * 64, (5 + j) * 64)])
            put(_f1(W, c))
            c2 = np.concatenate([j * 64 + sw, (4 + j) * 64 + sw])
            put(_f1(W, c2))
        put(_f1(W, np.arange(512, 640)))
        put(_f1(W, np.concatenate([512 + sw, 576 + sw])))
        assert b % 4 == 0
        vb = W[:, 1792:2304].reshape(8, 128, 512).transpose(1, 0, 2).reshape(128, 4096)
        for k in range(4):
            put(vb[:, k * 1024:(k + 1) * 1024])
        put(_f1(W, np.arange(640, 768)))
        for j in range(4):
            put(_f1(W, np.arange(768 + j * 128, 768 + (j + 1) * 128)))
        for j in range(4):
            put(_f1(W, np.arange(1280 + j * 128, 1280 + (j + 1) * 128)))
        WA = inp["w_branch_a"][l]
        WB = inp["w_branch_b"][l]
        rows_a = np.concatenate([np.concatenate([np.arange(kc * 64, (kc + 1) * 64),
                                                 np.arange((4 + kc) * 64, (5 + kc) * 64)]) for kc in range(4)])
        for j in range(8):
            put(_f1(W, np.arange(2304 + j * 128, 2304 + (j + 1) * 128)))
            put(_f1(W, np.arange(3328 + j * 128, 3328 + (j + 1) * 128)))
            br = np.concatenate([_f1(WA[rows_a], np.arange(j * 128, (j + 1) * 128)),
                                 _f1(WB, np.arange(j * 128, (j + 1) * 128))], axis=1)
            put(br)
        WO = inp["w_mix_out"][l]
        for j in range(8):
            put(_f1(WO, np.arange(j * 128, (j + 1) * 128)))
        ffn_blocks(inp["w_ffn2_in"][l], inp["w_ffn2_out"][l])
        assert b == 187, b
    s = blocks.reshape(2 * NSLOT_L, SLOT, 128, 1024).transpose(0, 2, 1, 3).reshape(2 * NSLOT_L, 128, SLOT * 1024)
    return np.ascontiguousarray(s)


def _small_params(inp):
    sp = np.zeros((128, NCOLS), np.float32)
    norms = [inp["norm_ffn1"][0], inp["norm_mix"][0], inp["norm_ffn2"][0],
             inp["norm_ffn1"][1], inp["norm_mix"][1], inp["norm_ffn2"][1], inp["norm_final"]]
    for i, g in enumerate(norms):
        sp[:, C_NORM + i * 8:C_NORM + (i + 1) * 8] = g.reshape(8, 128).T
    for l in range(2):
        sp[:, C_BG + l * 16:C_BG + (l + 1) * 16] = inp["b_gate"][l].reshape(16, 128).T
        for j in range(4):
            sp[0:64, C_SINK + l * 4 + j] = inp["sinks"][l, j]
            sp[64:128, C_SINK + l * 4 + j] = inp["sinks"][l, 4 + j]
        for h in range(8):
            sp[:, C_BC + l * 8 + h] = inp["rel_bias"][l, h, 0]
    d = np.arange(128) % 64
    sp[:, C_RI] = np.where(d < 16, d % 8, 0)
    sp[:, C_RM] = (d < 16)
    sp[:, C_RS] = np.where(d < 8, -1.0, np.where(d < 16, 1.0, 0.0))
    sp[:, C_RO] = 1.0 - (d < 16)
    return sp


_CACHE = {}


def kernel(**inputs):
    inp = {k: np.asarray(v) for k, v in inputs.items()}
    if "nc" not in _CACHE:
        _CACHE["nc"] = build()[0]
    nc = _CACHE["nc"]
    wst = _weight_stream(inp)
    smallp = _small_params(inp)
    m = np.arange(385)
    ext_idx = np.maximum(m - 128, 0)
    rev = inp["rel_bias"][:, :, ext_idx][:, :, ::-1]
    relx = np.ascontiguousarray(rev.reshape(16, 385).T.reshape(1, 16 * 385)).astype(np.float32)
    ident = np.eye(128, dtype=np.float32)
    in_maps = []
    for c in range(8):
        b, half = c // 2, c % 2
        start = half * OWN - HALO
        xseg = np.zeros((SEG, D), np.float32)
        lo = max(start, 0)
        xseg[lo - start:] = inp["x_prompt"][b, lo:start + SEG]
        posv = np.arange(start, start + SEG, dtype=np.float32)
        vld = np.repeat((posv >= 0).astype(np.float32)[:, None], 64, axis=1)
        pos = np.concatenate([posv, np.tile(np.arange(PAST, PAST + 64, dtype=np.float32), NS)])[None, :]
        sl = slice(c * NS, (c + 1) * NS)
        in_maps.append({
            "xseg": xseg,
            "xsmp": np.ascontiguousarray(inp["x_sample"][sl].reshape(SN, D)),
            "pos": np.ascontiguousarray(pos),
            "vld": np.ascontiguousarray(vld),
            "wst": wst,
            "smallp": smallp,
            "relx": relx,
            "ident": ident,
            "ckw": np.ascontiguousarray(inp["cache_k_win"][:, sl].reshape(2, NS, 128, 128)),
            "cvw": np.ascontiguousarray(inp["cache_v_win"][:, sl].reshape(2, NS, 128, 128)),
            "ckb": np.ascontiguousarray(inp["cache_k_band"][:, sl].reshape(2, NS, 512, 512)),
            "cvb": np.ascontiguousarray(inp["cache_v_band"][:, sl].reshape(2, NS, 512, 512)),
        })
    if "dbg_cores" in _CACHE:
        sel = _CACHE["dbg_cores"]
        res = run_bass_kernel_spmd(nc, [in_maps[i] for i in sel], core_ids=list(range(len(sel))))
        R = [res.results[sel.index(i)] if i in sel else res.results[0] for i in range(8)]
    else:
        res = run_bass_kernel_spmd(nc, in_maps, core_ids=list(range(8)))
        R = res.results
    y_prompt = np.zeros((4, SEQ, D), np.float32)
    y_sample = np.zeros((32, 64, D), np.float32)
    kwp = np.zeros((2, 4, 128, 2, 64), np.float32)
    vwp = np.zeros((2, 4, 128, 2, 64), np.float32)
    kbp = np.zeros((2, 4, 512, 8, 64), np.float32)
    vbp = np.zeros((2, 4, 512, 8, 64), np.float32)
    kws = np.zeros((2, 32, 128, 2, 64), np.float32)
    vws = np.zeros((2, 32, 128, 2, 64), np.float32)
    kbs = np.zeros((2, 32, 512, 8, 64), np.float32)
    vbs = np.zeros((2, 32, 512, 8, 64), np.float32)
    for c in range(8):
        b, half = c // 2, c % 2
        r = R[c]
        y_prompt[b, half * OWN:(half + 1) * OWN] = r["y"]
        sl = slice(c * NS, (c + 1) * NS)
        y_sample[sl] = r["ys"].reshape(NS, 64, D)
        if half == 1:
            kwp[:, b] = r["kwp"].reshape(2, 128, 2, 64)
            vwp[:, b] = r["vwp"].reshape(2, 128, 2, 64)
            kbp[:, b] = r["kbp"].reshape(2, 512, 8, 64)
            vbp[:, b] = r["vbp"].reshape(2, 512, 8, 64)
        kws[:, sl] = r["kws"].reshape(2, NS, 128, 2, 64)
        vws[:, sl] = r["vws"].reshape(2, NS, 128, 2, 64)
        kbs[:, sl] = r["kbs"].reshape(2, NS, 512, 8, 64)
        vbs[:, sl] = r["vbs"].reshape(2, NS, 512, 8, 64)
    return (y_prompt, y_sample, kwp, vwp, kbp, vbp, kws, vws, kbs, vbs)
```
